# Optimizing a Trainium2 kernel written in Bass

```python
import jax, jax.numpy as jnp
from jax import lax
import numpy as np

D_MODEL = 1024
BATCH = 16
SEQ = 4096
DEPTH = 4

CHUNK = 64
N_MEM = 256
N_GROUPS = 4
GROUP = D_MODEL // N_GROUPS
RET_HEADS = 4
RET_HD = GROUP // RET_HEADS
ROPE_BASE = 10000.0
LRU_WIDTH = GROUP
LRU_BLOCKS = 4
LRU_BD = LRU_WIDTH // LRU_BLOCKS
CONV_W = 4
LRU_C = 8.0
GLA_HEADS = 4
GLA_DK = GROUP // 2
GLA_DV = GROUP
GLA_RANK = 16
GLA_TAU = 16.0
HGRN_HEADS = 4
HGRN_HD = GROUP // HGRN_HEADS
XA_HEADS = 4
XA_HD = D_MODEL // XA_HEADS
D_FF = 2816
EPS = 1e-6
IN_SPLITS = (GROUP, GROUP, GROUP, GROUP,
             LRU_WIDTH, LRU_WIDTH,
             GLA_DK, GLA_DK, GLA_DV, GLA_RANK, GLA_DV,
             GROUP, GROUP, GROUP, GROUP)
D_IN = 8 * GROUP + 2 * LRU_WIDTH + 2 * GLA_DK + 2 * GLA_DV + GLA_RANK

kernel_name = "hybrid_chunk_causal_parallel_groups"


def rms_norm(x, g):
    xf = x.astype(jnp.float32)
    y = xf * lax.rsqrt(jnp.mean(xf * xf, axis=-1, keepdims=True) + EPS)
    return (y * g.astype(jnp.float32)).astype(x.dtype)


def head_norm(o):
    of = o.astype(jnp.float32)
    mu = jnp.mean(of, axis=-1, keepdims=True)
    var = jnp.mean(jnp.square(of - mu), axis=-1, keepdims=True)
    return ((of - mu) * lax.rsqrt(var + EPS)).astype(o.dtype)


def swiglu(x, w_gu, w_down):
    g, u = jnp.split(x @ w_gu, 2, axis=-1)
    return (jax.nn.silu(g) * u) @ w_down


def to_heads(t, n_heads):
    b, s, _ = t.shape
    return t.reshape(b, s, n_heads, -1).transpose(0, 2, 1, 3)


def from_heads(t):
    b, h, s, d = t.shape
    return t.transpose(0, 2, 1, 3).reshape(b, s, h * d)


def rope(t, cos, sin):
    t1, t2 = jnp.split(t, 2, axis=-1)
    return jnp.concatenate([t1 * cos - t2 * sin, t1 * sin + t2 * cos], axis=-1)


def chunk_gated_linear_attn(q, k, v, log_f, causal_in_chunk):
    b_, h_, s_, dk = q.shape
    dv = v.shape[-1]
    nc = s_ // CHUNK

    def to_chunks(t):
        return jnp.moveaxis(t.reshape(b_, h_, nc, CHUNK, t.shape[-1]), 2, 0)

    pos = jnp.arange(CHUNK)
    if causal_in_chunk:
        mask = pos[:, None] >= pos[None, :]
    else:
        mask = jnp.ones((CHUNK, CHUNK), dtype=bool)

    def step(state, inp):
        qc, kc, vc, gc = (t.astype(jnp.float32) for t in inp)
        cum = jnp.cumsum(gc, axis=2)
        decay = jnp.exp(-jnp.abs(cum[:, :, :, None, :] - cum[:, :, None, :, :]))
        decay = jnp.where(mask[None, None, :, :, None], decay, 0.0)
        scores = jnp.einsum("bhjd,bhmd,bhjmd->bhjm", qc, kc, decay)
        out = (jnp.einsum("bhjm,bhme->bhje", scores, vc)
               + jnp.einsum("bhjd,bhde->bhje", qc * jnp.exp(cum), state))
        last = cum[:, :, -1, :]
        state = (jnp.exp(last)[..., None] * state
                 + jnp.einsum("bhmd,bhme->bhde", kc * jnp.exp(last[:, :, None, :] - cum), vc))
        return state, out

    state0 = jnp.zeros((b_, h_, dk, dv), jnp.float32)
    _, out = lax.scan(step, state0, (to_chunks(q), to_chunks(k), to_chunks(v), to_chunks(log_f)))
    return jnp.moveaxis(out, 0, 2).reshape(b_, h_, s_, dv).astype(v.dtype)


def retention_group(q, k, v, g, cos, sin):
    qh = rope(to_heads(q, RET_HEADS), cos, sin)
    kh = rope(to_heads(k, RET_HEADS), cos, sin) * (RET_HD ** -0.5)
    vh = to_heads(v, RET_HEADS)
    log_gamma = jnp.log1p(-jnp.exp2(-5.0 - jnp.arange(RET_HEADS, dtype=jnp.float32)))
    log_f = jnp.broadcast_to(log_gamma[None, :, None, None], qh.shape)
    o = chunk_gated_linear_attn(qh, kh, vh, log_f, causal_in_chunk=False)
    return jax.nn.silu(g) * from_heads(head_norm(o))


def rglru_group(xb, gate, conv_w, conv_b, wa, ba, wx, bx, lam):
    b_, s_, w_ = xb.shape
    xc = lax.conv_general_dilated(xb, conv_w[:, None, :], window_strides=(1,),
                                  padding=[(CONV_W - 1, 0)],
                                  dimension_numbers=("NWC", "WIO", "NWC"),
                                  feature_group_count=w_) + conv_b
    xblk = xc.reshape(b_, s_, LRU_BLOCKS, LRU_BD)
    r = jax.nn.sigmoid(jnp.einsum("bsnd,nde->bsne", xblk, wa).reshape(b_, s_, w_) + ba)
    i = jax.nn.sigmoid(jnp.einsum("bsnd,nde->bsne", xblk, wx).reshape(b_, s_, w_) + bx)
    log_a = (LRU_C * r.astype(jnp.float32)) * jax.nn.log_sigmoid(lam.astype(jnp.float32))
    a = jnp.exp(log_a)
    u = jnp.sqrt(-jnp.expm1(2.0 * log_a)) * (i * xc).astype(jnp.float32)

    def combine(left, right):
        a1, b1 = left
        a2, b2 = right
        return a1 * a2, a2 * b1 + b2

    _, h = lax.associative_scan(combine, (a, u), axis=1)
    return h.astype(xb.dtype) * jax.nn.gelu(gate)


def gla_group(q, k, v, a_lr, g, w_a2, b_a):
    qh = to_heads(q, GLA_HEADS) * ((GLA_DK // GLA_HEADS) ** -0.5)
    kh = to_heads(k, GLA_HEADS)
    vh = to_heads(v, GLA_HEADS)
    a_pre = (a_lr @ w_a2 + b_a).astype(jnp.float32)
    log_f = to_heads(jax.nn.log_sigmoid(a_pre) / GLA_TAU, GLA_HEADS)
    o = chunk_gated_linear_attn(qh, kh, vh, log_f, causal_in_chunk=False)
    return jax.nn.silu(g) * from_heads(head_norm(o))


def hgrn2_group(q, f_pre, i, g, lb):
    fp = f_pre.astype(jnp.float32)
    lbf = lb.astype(jnp.float32)
    log_f = jnp.logaddexp(jnp.log(lbf), jnp.log1p(-lbf) + jax.nn.log_sigmoid(fp))
    k = ((1.0 - lbf) * jax.nn.sigmoid(-fp)).astype(q.dtype)
    qh = to_heads(jax.nn.silu(q), HGRN_HEADS)
    o = chunk_gated_linear_attn(qh, to_heads(k, HGRN_HEADS), to_heads(i, HGRN_HEADS),
                                to_heads(log_f, HGRN_HEADS), causal_in_chunk=True)
    return jax.nn.silu(g) * from_heads(head_norm(o))


def cross_attention(h, m, wq, wkv, wo):
    qh = to_heads(h @ wq, XA_HEADS)
    k, v = jnp.split(m @ wkv, 2, axis=-1)
    kh = to_heads(k, XA_HEADS)
    vh = to_heads(v, XA_HEADS)
    s = jnp.einsum("bhqd,bhkd->bhqk", qh, kh).astype(jnp.float32) * (XA_HD ** -0.5)
    p = jax.nn.softmax(s, axis=-1).astype(vh.dtype)
    return from_heads(jnp.einsum("bhqk,bhkd->bhqd", p, vh)) @ wo


def setup_inputs(seed: int = 0) -> dict:
    key = jax.random.key(seed)
    ks = iter(jax.random.split(key, 64))
    L = DEPTH

    def nrm(shape, fan_in):
        return jax.random.normal(next(ks), shape, jnp.float32) * (fan_in ** -0.5)

    def gain(shape):
        return 1.0 + 0.02 * jax.random.normal(next(ks), shape, jnp.float32)

    def small(shape):
        return 0.01 * jax.random.normal(next(ks), shape, jnp.float32)

    x = jax.random.normal(next(ks), (BATCH, SEQ, D_MODEL), jnp.float32)
    mem = jax.random.normal(next(ks), (BATCH, N_MEM, D_MODEL), jnp.float32)
    u = jax.random.uniform(next(ks), (L, LRU_WIDTH), jnp.float32, minval=0.9, maxval=0.999)
    s = u ** (1.0 / LRU_C)
    lru_lambda = jnp.log(s) - jnp.log1p(-s)
    return {
        "x": x,
        "mem": mem,
        "ffn1_norm": gain((L, D_MODEL)),
        "ffn1_w_gu": nrm((L, D_MODEL, 2 * D_FF), D_MODEL),
        "ffn1_w_down": nrm((L, D_FF, D_MODEL), D_FF),
        "mix_norm": gain((L, D_MODEL)),
        "w_in": nrm((L, D_MODEL, D_IN), D_MODEL),
        "w_out": nrm((L, N_GROUPS * GROUP, D_MODEL), N_GROUPS * GROUP),
        "lru_conv_w": nrm((L, CONV_W, LRU_WIDTH), CONV_W),
        "lru_conv_b": small((L, LRU_WIDTH)),
        "lru_wa": nrm((L, LRU_BLOCKS, LRU_BD, LRU_BD), LRU_BD),
        "lru_ba": small((L, LRU_WIDTH)),
        "lru_wx": nrm((L, LRU_BLOCKS, LRU_BD, LRU_BD), LRU_BD),
        "lru_bx": small((L, LRU_WIDTH)),
        "lru_lambda": lru_lambda,
        "gla_w_a2": nrm((L, GLA_RANK, GLA_DK), GLA_RANK),
        "gla_b_a": small((L, GLA_DK)),
        "hgrn_lb_logits": 0.1 * jax.random.normal(next(ks), (L, GROUP), jnp.float32),
        "xattn_norm": gain((L, D_MODEL)),
        "mem_norm": gain((L, D_MODEL)),
        "xattn_wq": nrm((L, D_MODEL, D_MODEL), D_MODEL),
        "xattn_wkv": nrm((L, D_MODEL, 2 * D_MODEL), D_MODEL),
        "xattn_wo": nrm((L, D_MODEL, D_MODEL), D_MODEL),
        "ffn2_norm": gain((L, D_MODEL)),
        "ffn2_w_gu": nrm((L, D_MODEL, 2 * D_FF), D_MODEL),
        "ffn2_w_down": nrm((L, D_FF, D_MODEL), D_FF),
        "final_norm": gain((D_MODEL,)),
    }


def reference(x, mem, ffn1_norm, ffn1_w_gu, ffn1_w_down, mix_norm, w_in, w_out,
              lru_conv_w, lru_conv_b, lru_wa, lru_ba, lru_wx, lru_bx, lru_lambda,
              gla_w_a2, gla_b_a, hgrn_lb_logits, xattn_norm, mem_norm,
              xattn_wq, xattn_wkv, xattn_wo, ffn2_norm, ffn2_w_gu, ffn2_w_down, final_norm):
    seq = x.shape[1]
    inv_freq = ROPE_BASE ** (-jnp.arange(RET_HD // 2, dtype=jnp.float32) / (RET_HD // 2))
    ang = jnp.arange(seq, dtype=jnp.float32)[:, None] * inv_freq[None, :]
    cos = jnp.cos(ang).astype(x.dtype)
    sin = jnp.sin(ang).astype(x.dtype)
    lb_cum = jnp.cumsum(jax.nn.softmax(hgrn_lb_logits.astype(jnp.float32), axis=0), axis=0)
    lb_all = lb_cum - lb_cum[0:1]
    split_at = [int(c) for c in np.cumsum(IN_SPLITS)[:-1]]

    for l in range(DEPTH):
        x = x + 0.5 * swiglu(rms_norm(x, ffn1_norm[l]), ffn1_w_gu[l], ffn1_w_down[l])
        h = rms_norm(x, mix_norm[l])
        (rq, rk, rv, rg, lx, lg, gq, gk, gv, ga, gg, hq, hf, hi, hg) = jnp.split(h @ w_in[l], split_at, axis=-1)
        y = jnp.concatenate([
            retention_group(rq, rk, rv, rg, cos, sin),
            rglru_group(lx, lg, lru_conv_w[l], lru_conv_b[l], lru_wa[l], lru_ba[l],
                        lru_wx[l], lru_bx[l], lru_lambda[l]),
            gla_group(gq, gk, gv, ga, gg, gla_w_a2[l], gla_b_a[l]),
            hgrn2_group(hq, hf, hi, hg, lb_all[l]),
        ], axis=-1)
        x = x + y @ w_out[l]
        x = x + cross_attention(rms_norm(x, xattn_norm[l]), rms_norm(mem, mem_norm[l]),
                                xattn_wq[l], xattn_wkv[l], xattn_wo[l])
        x = x + 0.5 * swiglu(rms_norm(x, ffn2_norm[l]), ffn2_w_gu[l], ffn2_w_down[l])
    return rms_norm(x, final_norm)
```

```python
import numpy as np
import concourse.bass as bass
import concourse.mybir as mybir
from concourse.bass_utils import run_bass_kernel_spmd

F32 = mybir.dt.float32
BF16 = mybir.dt.bfloat16
ALU = mybir.AluOpType
AF = mybir.ActivationFunctionType

D = 1024
KC = 8
SEQ = 4096
DEPTH = 4
DFF = 2816
NF = DFF // 128
T = 512
NMEM = 256
EPS = 1e-6
NCORES = 8
SEQ_PER_CORE = 2
TILES_PER_SEQ = SEQ // T
PIECE = 2048
NSLOTS = 9
LOOKAHEAD = 6

O_RQ, O_RK, O_RV, O_RG = 0, 256, 512, 768
O_LX, O_LG = 1024, 1280
O_GQ, O_GK, O_GV, O_GA, O_GG = 1536, 1664, 1792, 2048, 2064
O_HQ, O_HF, O_HI, O_HG = 2320, 2576, 2832, 3088

P_FFN1, P_MIX, P_XA, P_MEM, P_FFN2 = 0, 8, 16, 24, 32
P_CONVW, P_CONVB, P_BA, P_BX, P_LAM, P_GBA, P_LB, P_FIN = 40, 48, 50, 52, 54, 56, 58, 60
NPAR = 68

C_AVGD = 0
C_BD64 = 128
C_ONES = 256
C_IDENT = 384
C_RESET = 512
C_DT = 1024
C_ML = 1280
C_MU = 1344
C_G1 = 1408
C_G2 = 1536
C_DEC = 1664
NCON = 1672


def _rot_cols(base):
    cols = []
    for h in range(4):
        for i in range(64):
            cols.append(base + h * 64 + (i + 32) % 64)
    return np.array(cols)


def _gla_pad_cols(base):
    cols = []
    for h in range(4):
        for i in range(64):
            cols.append(base + h * 32 + i if i < 32 else -1)
    return np.array(cols)


def _rng(a, n):
    return np.arange(a, a + n)


def layer_piece_specs():
    S = []

    def ffn(tag):
        for f in range(NF):
            cols = np.concatenate([_rng(f * 128, 128), _rng(DFF + f * 128, 128)])
            S.append((tag + "_w_gu", 0, 8, cols))
        for cp in range(4):
            for (k0, nk) in ((0, 8), (8, 8), (16, 6)):
                S.append((tag + "_w_down", k0, nk, _rng(cp * 256, 256)))

    ffn("ffn1")
    for cols in (_rng(O_RQ, 256), _rot_cols(O_RQ), _rng(O_RK, 256), _rot_cols(O_RK),
                 _rng(O_RV, 256), _rng(O_RG, 256),
                 _rng(O_LX, 256), _rng(O_LG, 256),
                 _gla_pad_cols(O_GQ), _gla_pad_cols(O_GK), _rng(O_GV, 256),
                 np.concatenate([_rng(O_GA, 16), -np.ones(240, dtype=np.int64)]), _rng(O_GG, 256),
                 _rng(O_HQ, 256), _rng(O_HF, 256), _rng(O_HI, 256), _rng(O_HG, 256)):
        S.append(("w_in", 0, 8, cols))
    for cp in range(4):
        S.append(("w_out", 0, 8, _rng(cp * 256, 256)))
    for cp in range(4):
        S.append(("xattn_wq", 0, 8, _rng(cp * 256, 256)))
    for cp in range(4):
        S.append(("xattn_wo", 0, 8, _rng(cp * 256, 256)))
    ffn("ffn2")
    return S


def kv_piece_specs():
    return [("xattn_wkv", 0, 8, _rng(cp * 256, 256)) for cp in range(8)]


def pack_piece(W, k0, nk, cols):
    out = np.zeros((128, PIECE), np.float32)
    valid = cols >= 0
    sub = np.zeros((nk * 128, 256), np.float32)
    sub[:, valid] = W[k0 * 128:(k0 + nk) * 128][:, cols[valid]]
    out[:, :nk * 256] = sub.reshape(nk, 128, 256).transpose(1, 0, 2).reshape(128, nk * 256)
    return out


def make_constants():
    c = np.zeros((128, NCON), np.float32)
    c[:, C_AVGD:C_AVGD + 128] = 1.0 / 1024.0
    bd = np.zeros((128, 128), np.float32)
    bd[:64, :64] = 1.0 / 64
    bd[64:, 64:] = 1.0 / 64
    c[:, C_BD64:C_BD64 + 128] = bd
    c[:, C_ONES:C_ONES + 128] = 1.0
    c[:, C_IDENT:C_IDENT + 128] = np.eye(128, dtype=np.float32)
    rm = np.ones((128, 512), np.float32)
    rm[:, ::64] = 0.0
    c[:, C_RESET:C_RESET + 512] = rm
    j = np.arange(64)
    gam = 1.0 - 2.0 ** (-5.0 - np.arange(4, dtype=np.float64))
    for h in range(4):
        Dm = gam[h] ** np.abs(j[:, None] - j[None, :]) * 0.125
        c[:64, C_DT + h * 64:C_DT + (h + 1) * 64] = Dm
    c[:64, C_ML:C_ML + 64] = (j[:, None] <= j[None, :]).astype(np.float32)
    c[:64, C_MU:C_MU + 64] = (j[:, None] > j[None, :]).astype(np.float32)
    for hp in range(2):
        for ph in range(2):
            h = hp * 2 + ph
            c[ph * 64:(ph + 1) * 64, C_G1 + hp * 64:C_G1 + (hp + 1) * 64] = (gam[h] ** (j + 1.0))[None, :]
            c[ph * 64:(ph + 1) * 64, C_G2 + hp * 64:C_G2 + (hp + 1) * 64] = (gam[h] ** (63.0 - j) * 0.125)[None, :]
            c[ph * 64:(ph + 1) * 64, C_DEC + hp] = gam[h] ** 64.0
    return c


def make_rope_tables():
    inv_freq = (np.float32(10000.0) ** (-np.arange(32, dtype=np.float32) / np.float32(32))).astype(np.float32)
    ang = (np.arange(SEQ, dtype=np.float32)[:, None] * inv_freq[None, :]).astype(np.float32)
    cos = np.cos(ang.astype(np.float64)).astype(np.float32).T
    sin = np.sin(ang.astype(np.float64)).astype(np.float32).T
    ct = np.concatenate([cos, cos, cos, cos], axis=0)
    st = np.concatenate([-sin, sin, -sin, sin], axis=0)
    return np.ascontiguousarray(np.stack([ct, st], axis=0))


def make_params(inp, l):
    p = np.zeros((128, NPAR), np.float32)

    def col8(v):
        return v.reshape(8, 128).T

    p[:, P_FFN1:P_FFN1 + 8] = col8(inp["ffn1_norm"][l])
    p[:, P_MIX:P_MIX + 8] = col8(inp["mix_norm"][l])
    p[:, P_XA:P_XA + 8] = col8(inp["xattn_norm"][l])
    p[:, P_MEM:P_MEM + 8] = col8(inp["mem_norm"][l])
    p[:, P_FFN2:P_FFN2 + 8] = col8(inp["ffn2_norm"][l])
    p[:, P_FIN:P_FIN + 8] = col8(inp["final_norm"])
    cw = inp["lru_conv_w"][l]
    for ch in range(2):
        for tap in range(4):
            p[:, P_CONVW + ch * 4 + tap] = cw[tap, ch * 128:(ch + 1) * 128]
    for name, off in (("lru_conv_b", P_CONVB), ("lru_ba", P_BA), ("lru_bx", P_BX), ("lru_lambda", P_LAM)):
        p[:, off:off + 2] = inp[name][l].reshape(2, 128).T
    ba = inp["gla_b_a"][l]
    bpad = np.zeros(256, np.float32)
    for h in range(4):
        bpad[h * 64:h * 64 + 32] = ba[h * 32:(h + 1) * 32]
    p[:, P_GBA:P_GBA + 2] = bpad.reshape(2, 128).T
    p[:, P_LB:P_LB + 2] = inp["hgrn_lb_logits"][l].reshape(2, 128).T
    return p


NWS = 768


def make_wsmall(inp, l):
    w = np.zeros((128, NWS), np.float32)
    for nm, off in (("lru_wa", 0), ("lru_wx", 256)):
        W = inp[nm][l]
        for ch in range(2):
            for b in range(2):
                n = ch * 2 + b
                w[b * 64:(b + 1) * 64, off + ch * 128 + b * 64: off + ch * 128 + (b + 1) * 64] = W[n]
    wa2 = inp["gla_w_a2"][l]
    for h in range(4):
        w[:16, 512 + h * 64: 512 + h * 64 + 32] = wa2[:, h * 32:(h + 1) * 32]
    return w


class TT:
    __slots__ = ("name", "last_w", "readers")

    def __init__(self, name):
        self.name = name
        self.last_w = None
        self.readers = {}


class _Ret:
    pass


class _Rec:
    def __init__(self):
        self.call = None

    def __getattr__(self, name):
        def f(*a, **k):
            assert self.call is None
            self.call = (name, a, k)
            return _Ret()
        return f


class Op:
    __slots__ = ("eng", "fn", "deps", "signal", "count", "token", "idx")

    def __init__(self, eng, fn):
        self.eng = eng
        self.fn = fn
        self.deps = []
        self.signal = False
        self.count = 0
        self.token = None
        self.idx = 0


COMPUTE = ("pe", "act", "dve", "pool")
ALLENG = ("pe", "act", "dve", "pool", "sp")


class Prog:
    def __init__(self):
        self.q = {e: [] for e in ALLENG}
        self.nops = 0

    def op(self, eng, fn, reads=(), writes=(), dma=None):
        rec = _Rec()
        fn(rec)
        name, a, k = rec.call
        o = Op(eng, (name, a, k))
        o.idx = self.nops
        self.nops += 1
        if dma is not None:
            o.token = dma
        deps = {}
        for t in reads:
            if t.last_w is not None:
                deps[id(t.last_w)] = t.last_w
        for t in writes:
            if t.last_w is not None:
                deps[id(t.last_w)] = t.last_w
            for r in t.readers.values():
                deps[id(r)] = r
        for d in deps.values():
            if d is o:
                continue
            if d.token is None and d.eng == eng == "pe":
                continue
            o.deps.append(d)
            d.signal = True
        for t in reads:
            key = eng if dma is None else ("dma", o.idx)
            t.readers[key] = o
        for t in writes:
            t.last_w = o
            t.readers = {}
        self.q[eng].append(o)
        return o

    def emit(self, nc, block, sems):
        for e in COMPUTE:
            c = 0
            for o in self.q[e]:
                if o.token is None and o.signal:
                    c += 1
                    o.count = c
        q = self.q

        def run(engname, e):
            known = {}
            for o in q[engname]:
                need = {}
                for d in o.deps:
                    if d.token is not None:
                        s, v = d.token
                    else:
                        s, v = sems[d.eng], d.count
                    k = id(s)
                    if known.get(k, 0) >= v:
                        continue
                    if k not in need or need[k][1] < v:
                        need[k] = (s, v)
                for k, (s, v) in need.items():
                    e.wait_ge(s, v)
                    known[k] = v
                name, a, k = o.fn
                ins = getattr(e, name)(*a, **k)
                if o.token is not None:
                    ins.then_inc(o.token[0], 16)
                elif o.signal:
                    ins.then_inc(sems[engname], 1)

        @block.tensor
        def _(e):
            run("pe", e)

        @block.scalar
        def _(e):
            run("act", e)

        @block.vector
        def _(e):
            run("dve", e)

        @block.gpsimd
        def _(e):
            run("pool", e)

        @block.sync
        def _(e):
            run("sp", e)


class DryProg:
    def __init__(self):
        self.nops = 0
        self.q = {e: [] for e in ALLENG}

    def op(self, eng, fn, reads=(), writes=(), dma=None):
        self.nops += 1
        return None


class V:
    __slots__ = ("ap", "tts", "slot")

    def __init__(self, ap, tts, slot=None):
        self.ap = ap
        self.tts = list(tts)
        self.slot = slot


class Arena:
    def __init__(self, tensor, ng, name):
        self.tensor = tensor
        self.ng = ng
        self.ptr = 0
        self.tts = [TT("%s%d" % (name, i)) for i in range(ng)]

    def alloc(self, nbytes_per_part, dt):
        ng = (nbytes_per_part + 1023) // 1024
        assert self.ptr + ng <= self.ng, "arena overflow %d+%d>%d" % (self.ptr, ng, self.ng)
        g0 = self.ptr
        self.ptr += ng
        ap = self.tensor[:, g0 * 512:(g0 + ng) * 512]
        if dt == F32:
            ap = ap.bitcast(F32)[:, 0:nbytes_per_part // 4]
        else:
            ap = ap[:, 0:nbytes_per_part // 2]
        return V(ap, self.tts[g0:g0 + ng])


    def sub(self, ng):
        assert self.ptr + ng <= self.ng, "arena overflow (sub) %d+%d>%d" % (self.ptr, ng, self.ng)
        c = Arena.__new__(Arena)
        c.tensor = self.tensor[:, self.ptr * 512:(self.ptr + ng) * 512]
        c.ng = ng
        c.ptr = 0
        c.tts = self.tts[self.ptr:self.ptr + ng]
        self.ptr += ng
        return c


def par(gens):
    gens = list(gens)
    while gens:
        for g in list(gens):
            try:
                yield next(g)
            except StopIteration:
                gens.remove(g)


class Stream:
    pass


N_FFN = NF + 12
NKV = 2 * DEPTH * 2


class Builder:
    def __init__(self, tiles_per_seq=TILES_PER_SEQ, n_layers=DEPTH, subl=("ffn1", "mix", "xa", "ffn2"),
                 mixers=("ret", "lru", "gla", "hgrn"), final_norm=True, n_streams=2):
        self.tps = tiles_per_seq
        self.n_layers = n_layers
        self.subl = subl
        self.mixers = mixers
        self.final_norm = final_norm
        self.n_streams = n_streams
        self.lspecs = layer_piece_specs()
        self.kspecs = kv_piece_specs()
        self.NPL = len(self.lspecs)
        self.NPIECES = DEPTH * self.NPL + DEPTH * 8
        self.KV0 = self.NPIECES

    def build(self):
        nc = bass.Bass("TRN2", target_bir_lowering=False)
        self.nc = nc
        dr = {}
        dr["xT"] = nc.dram_tensor("xT", [D, SEQ_PER_CORE * SEQ], F32, kind="ExternalInput").ap()
        dr["memT"] = nc.dram_tensor("memT", [SEQ_PER_CORE, D, NMEM], F32, kind="ExternalInput").ap()
        dr["wsrc"] = nc.dram_tensor("wsrc", [self.NPIECES * 128, PIECE], F32, kind="ExternalInput").ap()
        dr["consts"] = nc.dram_tensor("consts", [128, NCON], F32, kind="ExternalInput").ap()
        dr["rope"] = nc.dram_tensor("rope", [2, 128, SEQ], F32, kind="ExternalInput").ap()
        dr["params"] = nc.dram_tensor("params", [128, DEPTH * NPAR], F32, kind="ExternalInput").ap()
        dr["wsmall"] = nc.dram_tensor("wsmall", [128, DEPTH * NWS], F32, kind="ExternalInput").ap()
        dr["outT"] = nc.dram_tensor("outT", [D, SEQ_PER_CORE * SEQ], F32, kind="ExternalOutput").ap()
        dr["wbf"] = nc.dram_tensor("wbf", [(self.NPIECES + NKV) * 128, PIECE], BF16, kind="Internal").ap()
        self.dr = dr

        import contextlib
        with contextlib.ExitStack() as es:
            def sb(name, shape, dt):
                return es.enter_context(nc.sbuf_tensor(name, shape, dt))

            def sem(name):
                return es.enter_context(nc.semaphore(name))

            self.sems = {e: sem("s_" + e) for e in COMPUTE}
            self.slot_sems = [sem("slot%d" % i) for i in range(NSLOTS)]
            self.NREG = DEPTH * 3 + 1
            self.cast_sems = [sem("cast%d" % i) for i in range(self.NREG)]
            self.st_sems = [sem("st%d" % i) for i in range(2)]
            self.misc_sems = [sem("misc%d" % i) for i in range(8)]
            self.kvst_sem = sem("kvst")
            self.cf = sb("cf", [128, NCON], F32)
            self.cb = sb("cb", [128, 512], BF16)
            self.par = sb("par", [128, DEPTH * NPAR], F32)
            self.der = sb("der", [128, DEPTH * 16], F32)
            self.wsm = sb("wsm", [128, DEPTH * NWS], BF16)
            self.epsc = sb("epsc", [128, 2], F32)
            self.slots = sb("slots", [128, NSLOTS, PIECE], BF16)
            self.sx = [sb("x%d" % s, [128, KC, T], F32) for s in range(2)]
            self.sxn = [sb("xn%d" % s, [128, KC, T], BF16) for s in range(2)]
            self.scs = [sb("cs%d" % s, [128, 2, T], F32) for s in range(2)]
            self.sS = [sb("S%d" % s, [128, DEPTH * 6, 128], F32) for s in range(2)]
            self.shst = [sb("hst%d" % s, [128, DEPTH * 2], F32) for s in range(2)]
            self.shalo = [sb("halo%d" % s, [128, DEPTH * 2, 4], F32) for s in range(2)]
            NG_M, NG_F = 48, 26
            self.wm = sb("work_m", [128, NG_M * 512], BF16)
            self.wf = sb("work_f", [128, NG_F * 512], BF16)
            self.NG_M, self.NG_F = NG_M, NG_F
            self.ps = [es.enter_context(nc.psum_tensor("ps%d" % i, [128, 512], F32)) for i in range(8)]

            self.order = None
            self.unit_counts = None
            self.reset_state(DryProg())
            self.dry = True
            self.emit_all()
            self.unit_counts = self.rec_units
            self.reset_state(DryProg())
            self.emit_all()
            self.order = self.rec_order
            self.P = Prog()
            self.reset_state(self.P)
            self.dry = False
            self.emit_all()
            assert self.s_next == len(self.order)
            with nc.Block() as block:
                self.P.emit(nc, block, self.sems)
        return nc

    def reset_state(self, P):
        self.P = P
        self.misc_cnt = [0] * 8
        self.st_cnt = [0] * 2
        self.kvst_cnt = 0
        self.t_cf = TT("cf"); self.t_cb = TT("cb"); self.t_par = TT("par"); self.t_der = TT("der")
        self.t_wsm = TT("wsm"); self.t_epsc = TT("epsc")
        self.t_misc = [TT("misc%d" % i) for i in range(8)]
        self.t_slots = [TT("slot%d" % i) for i in range(NSLOTS)]
        self.t_wbf = [TT("wbf%d" % i) for i in range(self.NREG)]
        self.t_kvd = [TT("kvd%d" % i) for i in range(2)]
        self.t_ps = [TT("ps%d" % i) for i in range(8)]
        self.ar_m = Arena(self.wm, self.NG_M, "wm")
        self.ar_f = Arena(self.wf, self.NG_F, "wf")
        self.bank_rr = {"F": 0, "M": 0}
        self.streams = []
        for s in range(2):
            st = Stream()
            st.s = s
            st.x, st.xn, st.cs, st.S, st.hst, st.halo = self.sx[s], self.sxn[s], self.scs[s], self.sS[s], self.shst[s], self.shalo[s]
            st.t_x = [TT("x%d_%d" % (s, i)) for i in range(KC)]
            st.t_xn = [TT("xn%d_%d" % (s, i)) for i in range(KC)]
            st.t_cs = TT("cs%d" % s)
            st.t_S = [TT("S%d_%d" % (s, i)) for i in range(DEPTH * 6)]
            st.t_hst = [TT("hst%d_%d" % (s, i)) for i in range(DEPTH)]
            st.t_halo = [TT("halo%d_%d" % (s, i)) for i in range(DEPTH)]
            self.streams.append(st)
        self.rec_order = []
        self.rec_units = {}
        self.s_next = 0
        self.s_issued = 0
        self.free_slots = list(range(NSLOTS))
        self.slot_of = {}
        self.slot_cnt = [0] * NSLOTS

    BANKS = {"F": (0, 1, 2, 3), "M": (4, 5, 6, 7)}

    def bank(self, pool):
        i = self.bank_rr[pool]
        self.bank_rr[pool] = (i + 1) % 4
        return self.BANKS[pool][i]

    def mm(self, out, lhsT, rhs, start, stop, reads, writes, **kw):
        self.P.op("pe", lambda e: e.matmul(out, lhsT, rhs, start=start, stop=stop, **kw), reads, writes)

    def pcol(self, l, col):
        return self.par[:, l * NPAR + col: l * NPAR + col + 1]

    def dcol(self, l, col):
        return self.der[:, l * 16 + col: l * 16 + col + 1]

    def rsqrt_eps(self, out, in_, reads, writes):
        self.P.op("act", lambda e: e.activation(out=out, in_=in_, func=AF.Ln, bias=self.epsc[:, 0:1]),
                  list(reads) + [self.t_epsc], writes)
        self.P.op("act", lambda e: e.activation(out=out, in_=out, func=AF.Exp, scale=-0.5), writes, writes)

    def dma(self, eng, out, in_, reads, writes, semk):
        self.misc_cnt[semk] += 16
        tok = (self.misc_sems[semk], self.misc_cnt[semk])
        return self.P.op(eng, lambda e: e.dma_start(out=out, in_=in_), reads, list(writes) + [self.t_misc[semk]], dma=tok)

    def region_of(self, key):
        if key[0] == "L":
            _, l, i = key
            grp = 0 if i < N_FFN else (2 if i >= self.NPL - N_FFN else 1)
            return l * 3 + grp
        if key[0] == "kv":
            return DEPTH * 3
        return None

    def pidx_of(self, key):
        if key[0] == "L":
            return key[1] * self.NPL + key[2]
        if key[0] == "kv":
            return DEPTH * self.NPL + key[1] * 8 + key[2]
        _, seq, l, j = key
        return self.KV0 + (seq * DEPTH + l) * 2 + j

    def _issue(self, s):
        key = self.order[s]
        slot = self.free_slots.pop(0)
        self.slot_of[s] = slot
        self.slot_cnt[slot] += 16
        tok = (self.slot_sems[slot], self.slot_cnt[slot])
        pidx = self.pidx_of(key)
        dst = self.slots[:, slot, :]
        src = self.dr["wbf"][pidx * 128:(pidx + 1) * 128, :]
        rd = [self.t_wbf[self.region_of(key)]] if key[0] != "kvs" else [self.t_kvd[key[1]]]
        self.P.op("sp", lambda e: e.dma_start(out=dst, in_=src), rd, [self.t_slots[slot]], dma=tok)

    def _pump(self):
        if self.dry:
            return
        while (self.s_issued < len(self.order) and self.free_slots and self.s_issued < self.s_next + LOOKAHEAD):
            self._issue(self.s_issued)
            self.s_issued += 1

    def wnext(self, key):
        if self.dry:
            self.rec_order.append(key)
            return V(self.slots[:, 0, :], [self.t_slots[0]], slot=None)
        s = self.s_next
        assert self.order[s] == key, (self.order[s], key)
        self.s_next += 1
        if self.s_issued <= s:
            assert self.free_slots, "weight ring exhausted (too many pieces held)"
        self._pump()
        assert self.s_issued > s
        slot = self.slot_of.pop(s)
        return V(self.slots[:, slot, :], [self.t_slots[slot]], slot=slot)

    def wrel(self, w):
        if self.dry:
            return
        self.free_slots.append(w.slot)
        self._pump()

    def cast_regions(self, regions):
        P = self.P
        dr = self.dr
        CH = 4
        for region in regions:
            if region < DEPTH * 3:
                l, grp = region // 3, region % 3
                a = (0, N_FFN, self.NPL - N_FFN)[grp]
                b = (N_FFN, self.NPL - N_FFN, self.NPL)[grp]
                p0, p1 = l * self.NPL + a, l * self.NPL + b
            else:
                p0, p1 = DEPTH * self.NPL, self.NPIECES
            cnt = 0
            i = p0
            while i < p1:
                n = min(CH, p1 - i)
                cnt += 16
                tok = (self.cast_sems[region], cnt)
                dst = dr["wbf"][i * 128:(i + n) * 128, :]
                src = dr["wsrc"][i * 128:(i + n) * 128, :]
                P.op("pool", lambda e, dst=dst, src=src: e.dma_start(out=dst, in_=src), [], [self.t_wbf[region]], dma=tok)
                i += n

    def prologue(self):
        P = self.P
        dr = self.dr
        self.dma("sp", self.cf[:, :], dr["consts"][:, :], [], [self.t_cf], 0)
        self.dma("sp", self.par[:, :], dr["params"][:, :], [], [self.t_par], 1)
        P.op("pool", lambda e: e.memset(self.epsc[:, 0:1], EPS), [], [self.t_epsc])
        P.op("pool", lambda e: e.memset(self.epsc[:, 1:2], 1.0), [], [self.t_epsc])
        P.op("dve", lambda e: e.tensor_copy(out=self.cb[:, :], in_=self.cf[:, 0:512]), [self.t_cf], [self.t_cb])
        for st in self.streams:
            P.op("pool", lambda e, st=st: e.memset(st.S[:, :, :], 0.0), [], st.t_S)
            P.op("pool", lambda e, st=st: e.memset(st.hst[:, :], 0.0), [], st.t_hst)
            P.op("pool", lambda e, st=st: e.memset(st.halo[:, :, :], 0.0), [], st.t_halo)
        self.cast_regions([0, DEPTH * 3, 1, 2])
        ar = self.ar_m
        ar.ptr = 0
        for l in range(DEPTH):
            stg = ar.alloc(NWS * 4, F32)
            self.dma("sp", stg.ap, dr["wsmall"][:, l * NWS:(l + 1) * NWS], [], stg.tts, 2)
            dst = self.wsm[:, l * NWS:(l + 1) * NWS]
            P.op("dve", lambda e, dst=dst, src=stg.ap: e.tensor_copy(out=dst, in_=src), stg.tts, [self.t_wsm])
        tmp = ar.alloc(64 * 4, F32)
        ta = tmp.ap
        rd = [self.t_par] + tmp.tts
        wr = [self.t_der] + tmp.tts
        for l in range(DEPTH):
            src = self.par[:, l * NPAR + P_LB: l * NPAR + P_LB + 2]
            P.op("act", lambda e, o=ta[:, l * 2:l * 2 + 2], s=src: e.activation(out=o, in_=s, func=AF.Exp), rd, wr)
        P.op("dve", lambda e: e.tensor_add(out=ta[:, 8:10], in0=ta[:, 0:2], in1=ta[:, 2:4]), rd, wr)
        P.op("dve", lambda e: e.tensor_add(out=ta[:, 8:10], in0=ta[:, 8:10], in1=ta[:, 4:6]), rd, wr)
        P.op("dve", lambda e: e.tensor_add(out=ta[:, 8:10], in0=ta[:, 8:10], in1=ta[:, 6:8]), rd, wr)
        P.op("dve", lambda e: e.reciprocal(out=ta[:, 10:12], in_=ta[:, 8:10]), rd, wr)
        for l in range(DEPTH):
            P.op("dve", lambda e, l=l: e.tensor_mul(out=ta[:, 12 + 2 * l:14 + 2 * l], in0=ta[:, 2 * l:2 * l + 2],
                                                    in1=ta[:, 10:12]), rd, wr)
        P.op("dve", lambda e: e.memset(self.der[:, 0:2], 0.0), rd, wr)
        for l in range(1, DEPTH):
            P.op("dve", lambda e, l=l: e.tensor_add(out=self.der[:, l * 16:l * 16 + 2],
                                                    in0=self.der[:, (l - 1) * 16:(l - 1) * 16 + 2],
                                                    in1=ta[:, 12 + 2 * l:14 + 2 * l]), rd, wr)
        for l in range(DEPTH):
            b = l * 16
            P.op("dve", lambda e, b=b: e.tensor_scalar(out=self.der[:, b + 2:b + 4], in0=self.der[:, b:b + 2],
                                                       scalar1=-1.0, scalar2=1.0, op0=ALU.mult, op1=ALU.add), rd, wr)
            lam = self.par[:, l * NPAR + P_LAM: l * NPAR + P_LAM + 2]
            P.op("act", lambda e, l=l, lam=lam: e.activation(out=ta[:, 20 + 2 * l:22 + 2 * l], in_=lam, func=AF.Exp,
                                                             scale=-1.0), rd, wr)
            P.op("act", lambda e, l=l: e.activation(out=ta[:, 28 + 2 * l:30 + 2 * l], in_=ta[:, 20 + 2 * l:22 + 2 * l],
                                                    func=AF.Ln, bias=self.epsc[:, 1:2]), rd + [self.t_epsc], wr)
            P.op("dve", lambda e, l=l, b=b: e.tensor_scalar(out=self.der[:, b + 4:b + 6], in0=ta[:, 28 + 2 * l:30 + 2 * l],
                                                            scalar1=-8.0, scalar2=None, op0=ALU.mult), rd, wr)
            P.op("dve", lambda e, l=l, b=b: e.tensor_scalar(out=self.der[:, b + 6:b + 8], in0=ta[:, 28 + 2 * l:30 + 2 * l],
                                                            scalar1=-16.0, scalar2=None, op0=ALU.mult), rd, wr)
            gba = self.par[:, l * NPAR + P_GBA: l * NPAR + P_GBA + 2]
            P.op("dve", lambda e, b=b, gba=gba: e.tensor_scalar(out=self.der[:, b + 8:b + 10], in0=gba,
                                                                scalar1=-1.0, scalar2=None, op0=ALU.mult), rd, wr)

    def rmsnorm(self, st, l, gcol, ar, pool):
        P = self.P
        sq = ar.alloc(KC * T * 2, BF16)
        rstd = ar.alloc(T * 4, F32)
        for q4 in range(4):
            src = st.x[:, q4 * 2:(q4 + 1) * 2, :]
            dst = sq.ap[:, q4 * 2 * T:(q4 + 1) * 2 * T].rearrange("p (a b) -> p a b", b=T)
            P.op("act", lambda e, src=src, dst=dst: e.activation(out=dst, in_=src, func=AF.Square),
                 st.t_x[q4 * 2:(q4 + 1) * 2], sq.tts[q4 * 2:(q4 + 1) * 2])
        b = self.bank(pool)
        for kc in range(KC):
            self.mm(self.ps[b][:, :], self.cb[:, C_AVGD:C_AVGD + 128], sq.ap[:, kc * T:(kc + 1) * T],
                    kc == 0, kc == KC - 1, [self.t_cb, sq.tts[kc]], [self.t_ps[b]])
        self.rsqrt_eps(rstd.ap, self.ps[b][:, :], [self.t_ps[b]], rstd.tts)
        for kc in range(KC):
            g = self.pcol(l, gcol + kc)
            P.op("dve", lambda e, kc=kc, g=g: e.scalar_tensor_tensor(out=st.xn[:, kc, :], in0=st.x[:, kc, :], scalar=g,
                                                                    in1=rstd.ap, op0=ALU.mult, op1=ALU.mult),
                 [st.t_x[kc], self.t_par] + rstd.tts, [st.t_xn[kc]])

    def ffn(self, st, l, gcol, base_i):
        P = self.P
        ar = self.ar_f
        ar.ptr = 0
        if st.s == 0 and st.cur_k == 0 and gcol == P_FFN1 and l + 1 < DEPTH:
            self.cast_regions([(l + 1) * 3, (l + 1) * 3 + 1, (l + 1) * 3 + 2])
        h = ar.alloc(NF * T * 2, BF16)
        sgs = [ar.alloc(T * 2, BF16) for _ in range(2)]
        save = ar.ptr
        ar.ptr = 12
        self.rmsnorm(st, l, gcol, ar, "F")
        ar.ptr = save
        yield
        i = base_i
        for f in range(NF):
            w = self.wnext(("L", l, i)); i += 1
            bg, bu = self.bank("F"), self.bank("F")
            for j, b in ((0, bg), (1, bu)):
                for kc in range(KC):
                    self.mm(self.ps[b][:, :], w.ap[:, kc * 256 + j * 128: kc * 256 + (j + 1) * 128], st.xn[:, kc, :],
                            kc == 0, kc == KC - 1, w.tts + [st.t_xn[kc]], [self.t_ps[b]])
                if j == 0:
                    yield
            self.wrel(w)
            sg = sgs[f % 2]
            P.op("act", lambda e, sg=sg, bg=bg: e.activation(out=sg.ap, in_=self.ps[bg][:, :], func=AF.Silu),
                 [self.t_ps[bg]], sg.tts)
            P.op("dve", lambda e, sg=sg, bu=bu, f=f: e.tensor_tensor(out=h.ap[:, f * T:(f + 1) * T], in0=sg.ap,
                                                                   in1=self.ps[bu][:, :], op=ALU.mult),
                 [self.t_ps[bu]] + sg.tts, [h.tts[f]])
            yield
        for cp in range(4):
            b2 = [self.bank("F"), self.bank("F")]
            for (k0, nk) in ((0, 8), (8, 8), (16, 6)):
                w = self.wnext(("L", l, i)); i += 1
                for j in range(2):
                    for fk in range(k0, k0 + nk):
                        self.mm(self.ps[b2[j]][:, :], w.ap[:, (fk - k0) * 256 + j * 128:(fk - k0) * 256 + (j + 1) * 128],
                                h.ap[:, fk * T:(fk + 1) * T], fk == 0, fk == NF - 1,
                                w.tts + [h.tts[fk]], [self.t_ps[b2[j]]])
                    if j == 0:
                        yield
                self.wrel(w)
                yield
            for j in range(2):
                dc = cp * 2 + j
                P.op("dve", lambda e, dc=dc, b=b2[j]: e.scalar_tensor_tensor(out=st.x[:, dc, :], in0=self.ps[b][:, :],
                                                                             scalar=0.5, in1=st.x[:, dc, :],
                                                                             op0=ALU.mult, op1=ALU.add),
                     [self.t_ps[b2[j]], st.t_x[dc]], [st.t_x[dc]])
        yield

    def kv_prepass(self, seq):
        P = self.P
        ar = self.ar_m
        ar.ptr = 0
        memT = ar.alloc(KC * NMEM * 4, F32)
        msq = ar.alloc(KC * NMEM * 2, BF16)
        mrstd = ar.alloc(NMEM * 4, F32)
        memn = ar.alloc(KC * NMEM * 2, BF16)
        stg = [ar.alloc(PIECE * 2, BF16) for _ in range(4)]
        src = self.dr["memT"][seq].rearrange("(kc p) n -> p kc n", p=128)
        dst = memT.ap.rearrange("p (a b) -> p a b", b=NMEM)
        self.dma("sp", dst, src, [], memT.tts, 3)
        P.op("act", lambda e: e.activation(out=msq.ap, in_=memT.ap, func=AF.Square), memT.tts, msq.tts)
        b = self.bank("M")
        for kc in range(KC):
            self.mm(self.ps[b][:, 0:NMEM], self.cb[:, C_AVGD:C_AVGD + 128], msq.ap[:, kc * NMEM:(kc + 1) * NMEM],
                    kc == 0, kc == KC - 1, [self.t_cb] + msq.tts, [self.t_ps[b]])
        self.rsqrt_eps(mrstd.ap, self.ps[b][:, 0:NMEM], [self.t_ps[b]], mrstd.tts)
        for l in range(self.n_layers):
            for kc in range(KC):
                g = self.pcol(l, P_MEM + kc)
                P.op("dve", lambda e, kc=kc, g=g: e.scalar_tensor_tensor(
                    out=memn.ap[:, kc * NMEM:(kc + 1) * NMEM], in0=memT.ap[:, kc * NMEM:(kc + 1) * NMEM], scalar=g,
                    in1=mrstd.ap, op0=ALU.mult, op1=ALU.mult), memT.tts + mrstd.tts + [self.t_par], memn.tts)
            sk = stg[(l % 2) * 2]
            sv = stg[(l % 2) * 2 + 1]
            for cp in range(8):
                w = self.wnext(("kv", l, cp))
                if cp < 4:
                    for j in range(2):
                        dc = cp * 2 + j
                        b = self.bank("M")
                        for kc in range(KC):
                            self.mm(self.ps[b][:, 0:NMEM], w.ap[:, kc * 256 + j * 128: kc * 256 + (j + 1) * 128],
                                    memn.ap[:, kc * NMEM:(kc + 1) * NMEM], kc == 0, kc == KC - 1,
                                    w.tts + memn.tts, [self.t_ps[b]])
                        dstk = sk.ap[:, dc * NMEM:(dc + 1) * NMEM]
                        P.op("act", lambda e, dstk=dstk, b=b: e.activation(out=dstk, in_=self.ps[b][:, 0:NMEM], func=AF.Copy),
                             [self.t_ps[b]], sk.tts)
                else:
                    for mh in range(2):
                        b = self.bank("M")
                        for kc in range(KC):
                            self.mm(self.ps[b][:, 0:256], memn.ap[:, kc * NMEM + mh * 128: kc * NMEM + (mh + 1) * 128],
                                    w.ap[:, kc * 256:(kc + 1) * 256], kc == 0, kc == KC - 1,
                                    w.tts + memn.tts, [self.t_ps[b]])
                        dstv = sv.ap[:, mh * D + (cp - 4) * 256: mh * D + (cp - 3) * 256]
                        P.op("dve", lambda e, dstv=dstv, b=b: e.tensor_copy(out=dstv, in_=self.ps[b][:, 0:256]),
                             [self.t_ps[b]], sv.tts)
                self.wrel(w)
            for j, sg_ in ((0, sk), (1, sv)):
                pidx = self.pidx_of(("kvs", seq, l, j))
                self.kvst_cnt += 16
                tok = (self.kvst_sem, self.kvst_cnt)
                dstd = self.dr["wbf"][pidx * 128:(pidx + 1) * 128, :]
                P.op("act", lambda e, dstd=dstd, sg_=sg_: e.dma_start(out=dstd, in_=sg_.ap), sg_.tts,
                     [self.t_kvd[seq], self.t_misc[5]], dma=tok)

    def xattn(self, st, l, base_i):
        P = self.P
        ar = self.ar_m
        ar.ptr = 0
        self.rmsnorm(st, l, P_XA, ar, "M")
        ar.ptr = 0
        qT = ar.alloc(KC * T * 2, BF16)
        pT = ar.alloc(KC * T * 2, BF16)
        oT = ar.alloc(KC * T * 2, BF16)
        rden = [ar.alloc(T * 4, F32) for _ in range(4)]
        yield
        i = base_i
        for cp in range(4):
            w = self.wnext(("L", l, i)); i += 1
            for j in range(2):
                dc = cp * 2 + j
                b = self.bank("M")
                for kc in range(KC):
                    self.mm(self.ps[b][:, :], w.ap[:, kc * 256 + j * 128: kc * 256 + (j + 1) * 128], st.xn[:, kc, :],
                            kc == 0, kc == KC - 1, w.tts + [st.t_xn[kc]], [self.t_ps[b]])
                dst = qT.ap[:, dc * T:(dc + 1) * T]
                if j == 0:
                    P.op("act", lambda e, dst=dst, b=b: e.activation(out=dst, in_=self.ps[b][:, :], func=AF.Copy, scale=1.0 / 16),
                         [self.t_ps[b]], [qT.tts[dc]])
                else:
                    P.op("dve", lambda e, dst=dst, b=b: e.tensor_scalar(out=dst, in0=self.ps[b][:, :], scalar1=1.0 / 16,
                                                                        scalar2=None, op0=ALU.mult),
                         [self.t_ps[b]], [qT.tts[dc]])
            self.wrel(w)
            yield
        wk = self.wnext(("kvs", st.s, l, 0))
        wv = self.wnext(("kvs", st.s, l, 1))
        def head_task(h):
            for mh in range(2):
                b = self.bank("M")
                for d2 in range(2):
                    dc = h * 2 + d2
                    self.mm(self.ps[b][:, :], wk.ap[:, dc * NMEM + mh * 128: dc * NMEM + (mh + 1) * 128],
                            qT.ap[:, dc * T:(dc + 1) * T], d2 == 0, d2 == 1,
                            wk.tts + [qT.tts[dc]], [self.t_ps[b]])
                dst = pT.ap[:, (h * 2 + mh) * T:(h * 2 + mh + 1) * T]
                P.op("act", lambda e, dst=dst, b=b: e.activation(out=dst, in_=self.ps[b][:, :], func=AF.Exp),
                     [self.t_ps[b]], [pT.tts[h * 2 + mh]])
            yield
            bd = self.bank("M")
            rd_ = rden[h]
            for mh in range(2):
                self.mm(self.ps[bd][:, :], self.cb[:, C_ONES:C_ONES + 128], pT.ap[:, (h * 2 + mh) * T:(h * 2 + mh + 1) * T],
                        mh == 0, mh == 1, [self.t_cb, pT.tts[h * 2 + mh]], [self.t_ps[bd]])
            P.op("act", lambda e, bd=bd, rd_=rd_: e.activation(out=rd_.ap, in_=self.ps[bd][:, :], func=AF.Ln), [self.t_ps[bd]], rd_.tts)
            P.op("act", lambda e, rd_=rd_: e.activation(out=rd_.ap, in_=rd_.ap, func=AF.Exp, scale=-1.0), rd_.tts, rd_.tts)
            for ec in range(2):
                b = self.bank("M")
                for mh in range(2):
                    self.mm(self.ps[b][:, :], wv.ap[:, mh * D + h * 256 + ec * 128: mh * D + h * 256 + (ec + 1) * 128],
                            pT.ap[:, (h * 2 + mh) * T:(h * 2 + mh + 1) * T], mh == 0, mh == 1,
                            wv.tts + [pT.tts[h * 2 + mh]], [self.t_ps[b]])
                dst = oT.ap[:, (h * 2 + ec) * T:(h * 2 + ec + 1) * T]
                P.op("dve", lambda e, dst=dst, b=b, rd_=rd_: e.tensor_tensor(out=dst, in0=self.ps[b][:, :], in1=rd_.ap, op=ALU.mult),
                     [self.t_ps[b]] + rd_.tts, [oT.tts[h * 2 + ec]])
            yield
        yield from par([head_task(h) for h in range(4)])
        self.wrel(wk)
        self.wrel(wv)
        for cp in range(4):
            w = self.wnext(("L", l, i)); i += 1
            for j in range(2):
                dc = cp * 2 + j
                b = self.bank("M")
                for ec in range(KC):
                    self.mm(self.ps[b][:, :], w.ap[:, ec * 256 + j * 128: ec * 256 + (j + 1) * 128],
                            oT.ap[:, ec * T:(ec + 1) * T], ec == 0, ec == KC - 1, w.tts + [oT.tts[ec]], [self.t_ps[b]])
                P.op("dve", lambda e, dc=dc, b=b: e.tensor_tensor(out=st.x[:, dc, :], in0=self.ps[b][:, :],
                                                                  in1=st.x[:, dc, :], op=ALU.add),
                     [self.t_ps[b], st.t_x[dc]], [st.t_x[dc]])
            self.wrel(w)
            yield

    def projF(self, st, w, j):
        b = self.bank("M")
        for kc in range(KC):
            self.mm(self.ps[b][:, :], w.ap[:, kc * 256 + j * 128: kc * 256 + (j + 1) * 128], st.xn[:, kc, :],
                    kc == 0, kc == KC - 1, w.tts + [st.t_xn[kc]], [self.t_ps[b]])
        return b

    def projT(self, st, w, vt):
        P = self.P
        for c2 in range(4):
            b = self.bank("M")
            for cc in range(2):
                c = c2 * 2 + cc
                for kc in range(KC):
                    self.mm(self.ps[b][0:64, cc * 256:(cc + 1) * 256], st.xn[:, kc, c * 64:(c + 1) * 64],
                            w.ap[:, kc * 256:(kc + 1) * 256], kc == 0, kc == KC - 1,
                            w.tts + [st.t_xn[kc]], [self.t_ps[b]])
            dst = vt.ap[0:64, c2 * 512:(c2 + 1) * 512]
            if c2 % 2 == 0:
                P.op("act", lambda e, dst=dst, b=b: e.activation(out=dst, in_=self.ps[b][0:64, :], func=AF.Copy),
                     [self.t_ps[b]], vt.tts)
            else:
                P.op("dve", lambda e, dst=dst, b=b: e.tensor_copy(out=dst, in_=self.ps[b][0:64, :]),
                     [self.t_ps[b]], vt.tts)
            yield

    def gate_silu(self, st, w, j):
        b = self.projF(st, w, j)
        sg = self.ar_m.alloc(T * 2, BF16)
        self.P.op("act", lambda e: e.activation(out=sg.ap, in_=self.ps[b][:, :], func=AF.Silu), [self.t_ps[b]], sg.tts)
        return sg

    def bcast_chunk_last(self, ap2d):
        return ap2d.rearrange("p (c j) -> p c j", j=64)[:, :, 63:64].to_broadcast([128, 8, 64])

    def lin_attn_hp(self, st, pairs, QhT, KtT, vt, hp, dec_fn, s_idx, sg, ychunk, y, ar=None, inpar=False):
        P = self.P
        ar = ar or self.ar_m
        mark = ar.ptr
        kt_tok = ar.alloc(8 * 128 * 2, BF16)
        sbf = ar.alloc(8 * 128 * 2, BF16)
        sc_sb = ar.alloc(16 * 64 * 2, BF16)
        o_sb = ar.alloc(T * 4, F32)
        obf = ar.alloc(T * 2, BF16)
        dd = ar.alloc(T * 4, F32)
        d2 = ar.alloc(T * 2, BF16)
        rs = o_sb
        S = st.S[:, s_idx, :]
        tS = st.t_S[s_idx]
        ident = self.cb[:, C_IDENT:C_IDENT + 128]
        bt = self.bank("M")
        psb = self.ps[bt][:, :].bitcast(BF16)
        for c in range(8):
            P.op("pe", lambda e, c=c: e.transpose(out=psb[0:64, c * 128:(c + 1) * 128], in_=KtT.ap[:, c * 64:(c + 1) * 64],
                                                 identity=ident),
                 KtT.tts + [self.t_cb], [self.t_ps[bt]])
        P.op("act", lambda e: e.activation(out=kt_tok.ap[0:64, :], in_=psb[0:64, :], func=AF.Copy),
             [self.t_ps[bt]], kt_tok.tts)
        yield
        bU = [self.bank("M"), self.bank("M")]
        for c in range(8):
            b = bU[c // 4]
            self.mm(self.ps[b][:, (c % 4) * 128:(c % 4 + 1) * 128], kt_tok.ap[0:64, c * 128:(c + 1) * 128],
                    vt.ap[0:64, c * 256 + hp * 128: c * 256 + (hp + 1) * 128], True, True,
                    kt_tok.tts + vt.tts, [self.t_ps[b]])
        for c in range(8):
            b = bU[c // 4]
            P.op("act", lambda e, c=c: e.activation(out=sbf.ap[:, c * 128:(c + 1) * 128], in_=S, func=AF.Copy),
                 [tS], sbf.tts)
            dap, drd = dec_fn(c)
            P.op("dve", lambda e, c=c, b=b, dap=dap: e.scalar_tensor_tensor(
                out=S, in0=S, scalar=dap, in1=self.ps[b][:, (c % 4) * 128:(c % 4 + 1) * 128], op0=ALU.mult, op1=ALU.add),
                [tS, self.t_ps[b]] + drd, [tS])
            if c % 4 == 3 and (c == 7 or not inpar):
                yield 2
        first = True
        sc4 = sc_sb.ap[0:64, :].rearrange("p (c h j) -> p c h j", h=2, j=64)
        for (KT, QT, mask_fn) in pairs:
            bs = [self.bank("M"), self.bank("M")]
            for ph in range(2):
                b = bs[ph]
                for c in range(8):
                    self.mm(self.ps[b][0:64, c * 64:(c + 1) * 64], KT.ap[ph * 64:(ph + 1) * 64, c * 64:(c + 1) * 64],
                            QT.ap[ph * 64:(ph + 1) * 64, c * 64:(c + 1) * 64], True, True,
                            KT.tts + QT.tts, [self.t_ps[b]])
            for ph in range(2):
                b = bs[ph]
                src = self.ps[b][0:64, :].rearrange("p (c j) -> p c j", j=64)
                dst = sc4[:, :, ph, :]
                m = mask_fn(ph)
                if first:
                    P.op("dve", lambda e, src=src, dst=dst, m=m: e.tensor_tensor(out=dst, in0=src, in1=m, op=ALU.mult),
                         [self.t_ps[b], self.t_cf], sc_sb.tts)
                else:
                    tmp = ar.alloc(512 * 4, F32)
                    tv = tmp.ap[0:64, :].rearrange("p (c j) -> p c j", j=64)
                    P.op("dve", lambda e, src=src, tv=tv, m=m: e.tensor_tensor(out=tv, in0=src, in1=m, op=ALU.mult),
                         [self.t_ps[b], self.t_cf], tmp.tts)
                    P.op("pool", lambda e, dst=dst, tv=tv: e.tensor_tensor(out=dst, in0=dst, in1=tv, op=ALU.add),
                         tmp.tts + sc_sb.tts, sc_sb.tts)
            first = False
            yield
        bA = [self.bank("M"), self.bank("M")]
        bB1 = self.bank("M")
        for ph in range(2):
            for c in range(8):
                g = c * 2 + ph
                out = self.ps[bA[ph]][:, c * 64:(c + 1) * 64]
                self.mm(out, vt.ap[0:64, c * 256 + hp * 128: c * 256 + (hp + 1) * 128], sc_sb.ap[0:64, g * 64:(g + 1) * 64],
                        True, ph == 1, vt.tts + sc_sb.tts, [self.t_ps[bA[ph]]])
                if ph == 0:
                    self.mm(out, sbf.ap[0:64, c * 128:(c + 1) * 128], QhT.ap[0:64, c * 64:(c + 1) * 64],
                            False, True, sbf.tts + QhT.tts, [self.t_ps[bA[0]]])
        for c in range(8):
            self.mm(self.ps[bB1][:, c * 64:(c + 1) * 64], sbf.ap[64:128, c * 128:(c + 1) * 128], QhT.ap[64:128, c * 64:(c + 1) * 64],
                    True, True, sbf.tts + QhT.tts, [self.t_ps[bB1]])
        P.op("act", lambda e: e.activation(out=o_sb.ap[0:64, :], in_=self.ps[bA[0]][0:64, :], func=AF.Copy),
             [self.t_ps[bA[0]]], o_sb.tts)
        P.op("act", lambda e: e.activation(out=o_sb.ap[64:128, :], in_=self.ps[bA[1]][64:128, :], func=AF.Copy),
             [self.t_ps[bA[1]]], o_sb.tts)
        P.op("dve", lambda e: e.tensor_tensor(out=o_sb.ap[64:128, :], in0=o_sb.ap[64:128, :], in1=self.ps[bB1][64:128, :], op=ALU.add),
             [self.t_ps[bB1]] + o_sb.tts, o_sb.tts)
        P.op("dve", lambda e: e.tensor_copy(out=obf.ap, in_=o_sb.ap), o_sb.tts, obf.tts)
        yield
        bd64 = self.cb[:, C_BD64:C_BD64 + 128]
        bm = self.bank("M")
        self.mm(self.ps[bm][:, :], bd64, obf.ap, True, True, [self.t_cb] + obf.tts, [self.t_ps[bm]])
        P.op("dve", lambda e: e.tensor_tensor(out=dd.ap, in0=o_sb.ap, in1=self.ps[bm][:, :], op=ALU.subtract),
             o_sb.tts + [self.t_ps[bm]], dd.tts)
        P.op("act", lambda e: e.activation(out=d2.ap, in_=dd.ap, func=AF.Square), dd.tts, d2.tts)
        yield
        bv = self.bank("M")
        self.mm(self.ps[bv][:, :], bd64, d2.ap, True, True, [self.t_cb] + d2.tts, [self.t_ps[bv]])
        self.rsqrt_eps(rs.ap, self.ps[bv][:, :], [self.t_ps[bv]], rs.tts)
        P.op("dve", lambda e: e.tensor_tensor(out=dd.ap, in0=dd.ap, in1=rs.ap, op=ALU.mult), dd.tts + rs.tts, dd.tts)
        yd = y.ap[:, ychunk * T:(ychunk + 1) * T]
        P.op("pool", lambda e: e.tensor_tensor(out=yd, in0=dd.ap, in1=sg.ap, op=ALU.mult), dd.tts + sg.tts, [y.tts[ychunk]])
        ar.ptr = mark
        yield

    def mixer(self, st, l, base_i):
        P = self.P
        ar = self.ar_m
        ar.ptr = 0
        self.rmsnorm(st, l, P_MIX, ar, "M")
        ar.ptr = 0
        y = ar.alloc(KC * T * 2, BF16)
        if len(self.mixers) < 4:
            P.op("pool", lambda e: e.memset(y.ap, 0.0), [], y.tts)
        yield
        i = base_i
        mark0 = ar.ptr
        cos = st.cs[:, 0, :]
        sin = st.cs[:, 1, :]
        c3 = lambda ap: ap.rearrange("p (c j) -> p c j", j=64)
        if "ret" in self.mixers:
            ar.ptr = mark0
            rot = {}
            rtiles = {(nm, hp): ar.alloc(T * 2, BF16) for nm in ("q", "k") for hp in range(2)}
            mk_rope = ar.ptr
            for nm in ("q", "k"):
                ar.ptr = mk_rope
                w0 = self.wnext(("L", l, i)); i += 1
                b0 = [self.projF(st, w0, 0), self.projF(st, w0, 1)]
                self.wrel(w0)
                w1 = self.wnext(("L", l, i)); i += 1
                b1 = [self.projF(st, w1, 0), self.projF(st, w1, 1)]
                self.wrel(w1)
                for hp in range(2):
                    r = rtiles[(nm, hp)]
                    mkt = ar.ptr
                    t1 = ar.alloc(T * 4, F32)
                    t2 = ar.alloc(T * 4, F32)
                    ar.ptr = mkt if hp == 1 else ar.ptr
                    P.op("dve", lambda e, t1=t1, b=b0[hp]: e.tensor_tensor(out=t1.ap, in0=self.ps[b][:, :], in1=cos, op=ALU.mult),
                         [self.t_ps[b0[hp]], st.t_cs], t1.tts)
                    P.op("dve", lambda e, t2=t2, b=b1[hp]: e.tensor_tensor(out=t2.ap, in0=self.ps[b][:, :], in1=sin, op=ALU.mult),
                         [self.t_ps[b1[hp]], st.t_cs], t2.tts)
                    P.op("pool", lambda e, t1=t1, t2=t2, r=r: e.tensor_tensor(out=r.ap, in0=t1.ap, in1=t2.ap, op=ALU.add),
                         t1.tts + t2.tts, r.tts)
                    rot[(nm, hp)] = r
                yield
            ar.ptr = mk_rope
            wv = self.wnext(("L", l, i)); i += 1
            vt = ar.alloc(8 * 256 * 2, BF16)
            yield from self.projT(st, wv, vt)
            self.wrel(wv)
            wg = self.wnext(("L", l, i)); i += 1
            sgs_ = [self.gate_silu(st, wg, 0), self.gate_silu(st, wg, 1)]
            self.wrel(wg)
            yield

            def ret_task(hp, sub):
                sg = sgs_[hp]
                qh = sub.alloc(T * 2, BF16)
                kt = sub.alloc(T * 2, BF16)
                g1 = self.cf[:, C_G1 + hp * 64:C_G1 + (hp + 1) * 64].unsqueeze(1).to_broadcast([128, 8, 64])
                g2 = self.cf[:, C_G2 + hp * 64:C_G2 + (hp + 1) * 64].unsqueeze(1).to_broadcast([128, 8, 64])
                qr, kr = rot[("q", hp)], rot[("k", hp)]
                P.op("pool", lambda e: e.tensor_tensor(out=c3(qh.ap), in0=c3(qr.ap), in1=g1, op=ALU.mult),
                     qr.tts + [self.t_cf], qh.tts)
                P.op("pool", lambda e: e.tensor_tensor(out=c3(kt.ap), in0=c3(kr.ap), in1=g2, op=ALU.mult),
                     kr.tts + [self.t_cf], kt.tts)

                def mask_fn(ph):
                    h = hp * 2 + ph
                    return self.cf[0:64, C_DT + h * 64:C_DT + (h + 1) * 64].unsqueeze(1).to_broadcast([64, 8, 64])
                dec = self.cf[:, C_DEC + hp:C_DEC + hp + 1]
                yield
                yield from self.lin_attn_hp(st, [(kr, qr, mask_fn)], qh, kt, vt, hp, lambda c: (dec, [self.t_cf]),
                                            l * 6 + 0 + hp, sg, 0 + hp, y, ar=sub, inpar=True)
            subs = [ar.sub(14), ar.sub(14)]
            yield from par([ret_task(0, subs[0]), ret_task(1, subs[1])])
        else:
            i += 6
        if "lru" not in self.mixers:
            i += 2
        else:
            ar.ptr = mark0
            wx_ = self.wnext(("L", l, i)); i += 1
            wg_ = self.wnext(("L", l, i)); i += 1
            def lru_task(ch, sub):
                bx = self.projF(st, wx_, ch)
                xb = sub.alloc(516 * 4, F32)
                halo = st.halo[:, l * 2 + ch, 0:3]
                P.op("pool", lambda e, xb=xb, halo=halo: e.tensor_copy(out=xb.ap[:, 0:3], in_=halo), [st.t_halo[l]], xb.tts)
                P.op("act", lambda e, xb=xb, bx=bx: e.activation(out=xb.ap[:, 3:515], in_=self.ps[bx][:, :], func=AF.Copy),
                     [self.t_ps[bx]], xb.tts)
                P.op("pool", lambda e, xb=xb, halo=halo: e.tensor_copy(out=halo, in_=xb.ap[:, 512:515]), xb.tts, [st.t_halo[l]])
                xc = sub.alloc(T * 4, F32)
                cws = [self.pcol(l, P_CONVW + ch * 4 + tap) for tap in range(4)]
                P.op("pool", lambda e, xb=xb, xc=xc: e.tensor_scalar(out=xc.ap, in0=xb.ap[:, 3:515], scalar1=cws[3],
                                                                     scalar2=self.pcol(l, P_CONVB + ch), op0=ALU.mult, op1=ALU.add),
                     xb.tts + [self.t_par], xc.tts)
                for tap in range(3):
                    P.op("dve", lambda e, xb=xb, xc=xc, tap=tap: e.scalar_tensor_tensor(
                        out=xc.ap, in0=xb.ap[:, tap:tap + 512], scalar=cws[tap], in1=xc.ap, op0=ALU.mult, op1=ALU.add),
                        xb.tts + xc.tts + [self.t_par], xc.tts)
                xcb = sub.alloc(T * 2, BF16)
                P.op("act", lambda e, xc=xc, xcb=xcb: e.activation(out=xcb.ap, in_=xc.ap, func=AF.Copy), xc.tts, xcb.tts)
                yield 3
                br, bi = self.bank("M"), self.bank("M")
                wa = self.wsm[:, l * NWS + ch * 128: l * NWS + (ch + 1) * 128]
                wxm = self.wsm[:, l * NWS + 256 + ch * 128: l * NWS + 256 + (ch + 1) * 128]
                self.mm(self.ps[br][:, :], wa, xcb.ap, True, True, [self.t_wsm] + xcb.tts, [self.t_ps[br]])
                self.mm(self.ps[bi][:, :], wxm, xcb.ap, True, True, [self.t_wsm] + xcb.tts, [self.t_ps[bi]])
                r = sub.alloc(T * 4, F32)
                ig = sub.alloc(T * 4, F32)
                P.op("act", lambda e, r=r, br=br: e.activation(out=r.ap, in_=self.ps[br][:, :], func=AF.Sigmoid,
                                                               bias=self.pcol(l, P_BA + ch)), [self.t_ps[br], self.t_par], r.tts)
                P.op("act", lambda e, ig=ig, bi=bi: e.activation(out=ig.ap, in_=self.ps[bi][:, :], func=AF.Sigmoid,
                                                                 bias=self.pcol(l, P_BX + ch)), [self.t_ps[bi], self.t_par], ig.tts)
                a = sub.alloc(T * 4, F32)
                s = sub.alloc(T * 4, F32)
                P.op("act", lambda e, a=a, r=r: e.activation(out=a.ap, in_=r.ap, func=AF.Exp, scale=self.dcol(l, 4 + ch)),
                     r.tts + [self.t_der], a.tts)
                P.op("act", lambda e, s=s, r=r: e.activation(out=s.ap, in_=r.ap, func=AF.Exp, scale=self.dcol(l, 6 + ch)),
                     r.tts + [self.t_der], s.tts)
                P.op("act", lambda e, s=s: e.activation(out=s.ap, in_=s.ap, func=AF.Sqrt, scale=-1.0, bias=self.epsc[:, 1:2]),
                     s.tts + [self.t_epsc], s.tts)
                P.op("pool", lambda e, ig=ig, xc=xc: e.tensor_tensor(out=ig.ap, in0=ig.ap, in1=xc.ap, op=ALU.mult),
                     ig.tts + xc.tts, ig.tts)
                P.op("pool", lambda e, ig=ig, s=s: e.tensor_tensor(out=ig.ap, in0=ig.ap, in1=s.ap, op=ALU.mult),
                     ig.tts + s.tts, ig.tts)
                yield
                hcol = st.hst[:, l * 2 + ch: l * 2 + ch + 1]
                hh = sub.alloc(T * 4, F32)
                P.op("dve", lambda e, hh=hh, a=a, ig=ig, hcol=hcol: e.tensor_tensor_scan(
                    out=hh.ap, data0=a.ap, data1=ig.ap, initial=hcol, op0=ALU.mult, op1=ALU.add),
                    a.tts + ig.tts + [st.t_hst[l]], hh.tts)
                P.op("dve", lambda e, hh=hh, hcol=hcol: e.tensor_copy(out=hcol, in_=hh.ap[:, 511:512]), hh.tts, [st.t_hst[l]])
                bg = self.projF(st, wg_, ch)
                gx = sub.alloc(T * 4, F32)
                t3 = sub.alloc(T * 4, F32)
                P.op("act", lambda e, gx=gx, bg=bg: e.activation(out=gx.ap, in_=self.ps[bg][:, :], func=AF.Copy),
                     [self.t_ps[bg]], gx.tts)
                P.op("pool", lambda e, gx=gx, t3=t3: e.tensor_tensor(out=t3.ap, in0=gx.ap, in1=gx.ap, op=ALU.mult), gx.tts, t3.tts)
                P.op("pool", lambda e, t3=t3: e.tensor_scalar(out=t3.ap, in0=t3.ap, scalar1=0.044715, scalar2=1.0,
                                                              op0=ALU.mult, op1=ALU.add), t3.tts, t3.tts)
                P.op("pool", lambda e, gx=gx, t3=t3: e.tensor_tensor(out=t3.ap, in0=t3.ap, in1=gx.ap, op=ALU.mult),
                     t3.tts + gx.tts, t3.tts)
                P.op("act", lambda e, t3=t3: e.activation(out=t3.ap, in_=t3.ap, func=AF.Sigmoid, scale=1.5957691216057308),
                     t3.tts, t3.tts)
                P.op("pool", lambda e, gx=gx, t3=t3: e.tensor_tensor(out=t3.ap, in0=t3.ap, in1=gx.ap, op=ALU.mult),
                     t3.tts + gx.tts, t3.tts)
                yd = y.ap[:, (2 + ch) * T:(3 + ch) * T]
                P.op("pool", lambda e, hh=hh, t3=t3, yd=yd: e.tensor_tensor(out=yd, in0=hh.ap, in1=t3.ap, op=ALU.mult),
                     hh.tts + t3.tts, [y.tts[2 + ch]])
                yield
            subs = [ar.sub(20), ar.sub(20)]
            yield from par([lru_task(0, subs[0]), lru_task(1, subs[1])])
            self.wrel(wx_)
            self.wrel(wg_)
        for nm in ("gla", "hgrn"):
            if nm not in self.mixers:
                i += 5 if nm == "gla" else 4
                continue
            ar.ptr = mark0
            gla = nm == "gla"
            wq = self.wnext(("L", l, i)); i += 1
            wk = self.wnext(("L", l, i)); i += 1
            wv = self.wnext(("L", l, i)); i += 1
            vt = ar.alloc(8 * 256 * 2, BF16)
            yield from self.projT(st, wv, vt)
            self.wrel(wv)
            if gla:
                wa_ = self.wnext(("L", l, i)); i += 1
                ba_ = self.projF(st, wa_, 0)
                self.wrel(wa_)
                alr = ar.alloc(T * 2, BF16)
                P.op("act", lambda e: e.activation(out=alr.ap[0:16, :], in_=self.ps[ba_][0:16, :], func=AF.Copy),
                     [self.t_ps[ba_]], alr.tts)
            wg = self.wnext(("L", l, i)); i += 1
            es = -1.0 / 16 if gla else 1.0
            for hp in range(2):
                mk = ar.ptr
                sg = self.gate_silu(st, wg, hp)
                bq = self.projF(st, wq, hp)
                bk = self.projF(st, wk, hp)
                if hp == 1:
                    self.wrel(wg); self.wrel(wq); self.wrel(wk)
                B = ar.alloc(T * 4, F32)
                lf = ar.alloc(T * 4, F32)
                if gla:
                    bp = self.bank("M")
                    wa2 = self.wsm[0:16, l * NWS + 512 + hp * 128: l * NWS + 512 + (hp + 1) * 128]
                    self.mm(self.ps[bp][:, :], wa2, alr.ap[0:16, :], True, True, [self.t_wsm] + alr.tts, [self.t_ps[bp]])
                    P.op("act", lambda e: e.activation(out=lf.ap, in_=self.ps[bp][:, :], func=AF.Exp, scale=-1.0,
                                                       bias=self.dcol(l, 8 + hp)), [self.t_ps[bp], self.t_der], lf.tts)
                    P.op("act", lambda e: e.activation(out=lf.ap, in_=lf.ap, func=AF.Ln, bias=self.epsc[:, 1:2]),
                         lf.tts + [self.t_epsc], lf.tts)
                    kk = None
                else:
                    kk = ar.alloc(T * 4, F32)
                    P.op("act", lambda e: e.activation(out=lf.ap, in_=self.ps[bk][:, :], func=AF.Sigmoid),
                         [self.t_ps[bk]], lf.tts)
                    P.op("dve", lambda e: e.tensor_scalar(out=lf.ap, in0=lf.ap, scalar1=self.dcol(l, 2 + hp),
                                                          scalar2=self.dcol(l, 0 + hp), op0=ALU.mult, op1=ALU.add),
                         lf.tts + [self.t_der], lf.tts)
                    P.op("pool", lambda e: e.tensor_scalar(out=kk.ap, in0=lf.ap, scalar1=-1.0, scalar2=1.0,
                                                           op0=ALU.mult, op1=ALU.add), lf.tts, kk.tts)
                    P.op("act", lambda e: e.activation(out=lf.ap, in_=lf.ap, func=AF.Ln), lf.tts, lf.tts)
                reset = self.cf[:, C_RESET:C_RESET + 512]
                P.op("dve", lambda e: e.tensor_tensor_scan(out=B.ap, data0=reset, data1=lf.ap, initial=0.0,
                                                           op0=ALU.mult, op1=ALU.add), lf.tts + [self.t_cf], B.tts)
                yield 2
                E = ar.alloc(T * 4, F32)
                dl = lf
                P.op("act", lambda e: e.activation(out=E.ap, in_=B.ap, func=AF.Exp, scale=es), B.tts, E.tts)
                P.op("pool", lambda e: e.tensor_tensor(out=c3(dl.ap), in0=c3(B.ap), in1=self.bcast_chunk_last(B.ap),
                                                       op=ALU.subtract), B.tts, dl.tts)
                if not gla:
                    P.op("dve", lambda e: e.tensor_scalar(out=dl.ap, in0=dl.ap, scalar1=80.0, scalar2=None, op0=ALU.min),
                         dl.tts, dl.tts)
                Eq = ar.alloc(T * 4, F32)
                Ek = ar.alloc(T * 4, F32)
                P.op("act", lambda e: e.activation(out=Eq.ap, in_=dl.ap, func=AF.Exp, scale=es), dl.tts, Eq.tts)
                P.op("act", lambda e: e.activation(out=Ek.ap, in_=dl.ap, func=AF.Exp, scale=-es), dl.tts, Ek.tts)
                Qp = ar.alloc(T * 2, BF16)
                Qh = ar.alloc(T * 2, BF16)
                Km = ar.alloc(T * 2, BF16)
                if gla:
                    qsc = 32.0 ** -0.5
                    Qm = ar.alloc(T * 2, BF16)
                    Kp = ar.alloc(T * 2, BF16)
                    for (dst, fac) in ((Qp, Eq), (Qm, Ek), (Qh, E)):
                        P.op("dve", lambda e, dst=dst, fac=fac: e.scalar_tensor_tensor(
                            out=dst.ap, in0=self.ps[bq][:, :], scalar=qsc, in1=fac.ap, op0=ALU.mult, op1=ALU.mult),
                            [self.t_ps[bq]] + fac.tts, dst.tts)
                    for (dst, fac) in ((Km, Ek), (Kp, Eq)):
                        P.op("dve", lambda e, dst=dst, fac=fac: e.tensor_tensor(
                            out=dst.ap, in0=self.ps[bk][:, :], in1=fac.ap, op=ALU.mult),
                            [self.t_ps[bk]] + fac.tts, dst.tts)
                    ml = lambda ph: self.cf[0:64, C_ML:C_ML + 64].unsqueeze(1).to_broadcast([64, 8, 64])
                    mu = lambda ph: self.cf[0:64, C_MU:C_MU + 64].unsqueeze(1).to_broadcast([64, 8, 64])
                    pairs = [(Km, Qp, ml), (Kp, Qm, mu)]
                else:
                    qs = ar.alloc(T * 4, F32)
                    P.op("act", lambda e: e.activation(out=qs.ap, in_=self.ps[bq][:, :], func=AF.Silu),
                         [self.t_ps[bq]], qs.tts)
                    P.op("dve", lambda e: e.tensor_tensor(out=Qp.ap, in0=qs.ap, in1=Eq.ap, op=ALU.mult),
                         qs.tts + Eq.tts, Qp.tts)
                    P.op("dve", lambda e: e.tensor_tensor(out=Qh.ap, in0=qs.ap, in1=E.ap, op=ALU.mult),
                         qs.tts + E.tts, Qh.tts)
                    P.op("pool", lambda e: e.tensor_tensor(out=Km.ap, in0=kk.ap, in1=Ek.ap, op=ALU.mult),
                         kk.tts + Ek.tts, Km.tts)
                    ml = lambda ph: self.cf[0:64, C_ML:C_ML + 64].unsqueeze(1).to_broadcast([64, 8, 64])
                    pairs = [(Km, Qp, ml)]
                dec_fn = lambda c, E=E: (E.ap[:, c * 64 + 63: c * 64 + 64], E.tts)
                yield 3
                yield from self.lin_attn_hp(st, pairs, Qh, Km, vt, hp, dec_fn, l * 6 + (2 if gla else 4) + hp, sg,
                                            (4 if gla else 6) + hp, y)
                ar.ptr = mk
        for cp in range(4):
            w = self.wnext(("L", l, i)); i += 1
            for j in range(2):
                dc = cp * 2 + j
                b = self.bank("M")
                for ec in range(KC):
                    self.mm(self.ps[b][:, :], w.ap[:, ec * 256 + j * 128: ec * 256 + (j + 1) * 128],
                            y.ap[:, ec * T:(ec + 1) * T], ec == 0, ec == KC - 1, w.tts + [y.tts[ec]], [self.t_ps[b]])
                P.op("dve", lambda e, dc=dc, b=b: e.tensor_tensor(out=st.x[:, dc, :], in0=self.ps[b][:, :],
                                                                  in1=st.x[:, dc, :], op=ALU.add),
                     [self.t_ps[b], st.t_x[dc]], [st.t_x[dc]])
            self.wrel(w)
            yield

    def load_tile(self, st, k):
        st.cur_k = k
        tok0 = st.s * SEQ + k * T
        src = self.dr["xT"][:, tok0:tok0 + T].rearrange("(kc p) t -> p kc t", p=128)
        self.dma("sp", st.x[:, 0:4, :], src[:, 0:4, :], [], st.t_x[0:4], st.s)
        self.dma("sp", st.x[:, 4:8, :], src[:, 4:8, :], [], st.t_x[4:8], 6 + st.s)
        if "mix" in self.subl and "ret" in self.mixers:
            self.dma("sp", st.cs[:, :, :], self.dr["rope"][:, :, k * T:(k + 1) * T].rearrange("a p t -> p a t"), [],
                     [st.t_cs], 2 + st.s)
        yield

    def final_store(self, st, k):
        P = self.P
        ar = self.ar_f
        ar.ptr = 0
        tok0 = st.s * SEQ + k * T
        dst = self.dr["outT"][:, tok0:tok0 + T].rearrange("(kc p) t -> p kc t", p=128)
        if self.final_norm:
            ob = ar.alloc(KC * T * 4, F32)
            sq = ar.alloc(KC * T * 2, BF16)
            rstd = ar.alloc(T * 4, F32)
            for half in range(2):
                src = st.x[:, half * 4:(half + 1) * 4, :]
                d_ = sq.ap[:, half * 4 * T:(half + 1) * 4 * T].rearrange("p (a b) -> p a b", b=T)
                P.op("act", lambda e, src=src, d_=d_: e.activation(out=d_, in_=src, func=AF.Square),
                     st.t_x[half * 4:(half + 1) * 4], sq.tts[half * 4:(half + 1) * 4])
            b = self.bank("F")
            for kc in range(KC):
                self.mm(self.ps[b][:, :], self.cb[:, C_AVGD:C_AVGD + 128], sq.ap[:, kc * T:(kc + 1) * T],
                        kc == 0, kc == KC - 1, [self.t_cb, sq.tts[kc]], [self.t_ps[b]])
            self.rsqrt_eps(rstd.ap, self.ps[b][:, :], [self.t_ps[b]], rstd.tts)
            for kc in range(KC):
                g = self.pcol(0, P_FIN + kc)
                P.op("dve", lambda e, kc=kc, g=g: e.scalar_tensor_tensor(out=ob.ap[:, kc * T:(kc + 1) * T], in0=st.x[:, kc, :],
                                                                        scalar=g, in1=rstd.ap, op0=ALU.mult, op1=ALU.mult),
                     [st.t_x[kc], self.t_par] + rstd.tts, ob.tts[kc * 2:kc * 2 + 2])
            srcap = ob.ap.rearrange("p (a b) -> p a b", b=T)
            rd = ob.tts
        else:
            srcap = st.x[:, :, :]
            rd = st.t_x
        self.st_cnt[st.s] += 16
        P.op("act", lambda e: e.dma_start(out=dst, in_=srcap), rd, [], dma=(self.st_sems[st.s], self.st_cnt[st.s]))
        yield

    def stream_phases(self, st):
        n_ffn = N_FFN
        base_mix = n_ffn
        base_xa = n_ffn + 17 + 4
        base_f2 = base_xa + 8
        phases = []

        def chain(*gens):
            def run():
                for g in gens:
                    yield from g()
            return run

        cur = [lambda st=st: self.load_tile(st, 0)]
        for k in range(self.tps):
            for l in range(self.n_layers):
                if "ffn1" in self.subl:
                    cur.append(lambda st=st, l=l: self.ffn(st, l, P_FFN1, 0))
                phases.append(("P", chain(*cur)))
                Ls = []
                if "mix" in self.subl:
                    Ls.append(lambda st=st, l=l: self.mixer(st, l, base_mix))
                if "xa" in self.subl:
                    Ls.append(lambda st=st, l=l: self.xattn(st, l, base_xa))
                phases.append(("L", chain(*Ls)))
                cur = []
                if "ffn2" in self.subl:
                    cur.append(lambda st=st, l=l: self.ffn(st, l, P_FFN2, base_f2))
            cur.append(lambda st=st, k=k: self.final_store(st, k))
            if k + 1 < self.tps:
                cur.append(lambda st=st, k=k: self.load_tile(st, k + 1))
        phases.append(("P", chain(*cur)))
        return phases

    def run_all(self, key, gen):
        n = 0.0
        for w in gen:
            n += (w or 1)
        if self.dry and self.unit_counts is None:
            self.rec_units[key] = max(n, 1)

    def interleave(self, keyL, gL, keyP, gP):
        if self.dry and self.unit_counts is None:
            self.run_all(keyL, gL)
            self.run_all(keyP, gP)
            return
        nL, nP = self.unit_counts[keyL], self.unit_counts[keyP]
        acc = 0.0
        doneL = doneP = False
        while not (doneL and doneP):
            if not doneL:
                w = 1
                try:
                    w = next(gL) or 1
                except StopIteration:
                    doneL = True
                acc += w * nP / nL
            else:
                acc = 1e9
            while acc >= 1.0 and not doneP:
                try:
                    next(gP)
                except StopIteration:
                    doneP = True
                acc -= 1.0
            if doneP and not doneL:
                acc = 0.0

    def emit_all(self):
        P = self.P
        self.prologue()
        ph = [self.stream_phases(self.streams[s]) for s in range(self.n_streams)]
        if self.n_streams == 1:
            for i, (t, f) in enumerate(ph[0]):
                self.run_all((0, i), f())
                if i == 0 and "xa" in self.subl:
                    self.kv_prepass(0)
        else:
            A, B = ph
            self.run_all((0, 0), A[0][1]())
            if "xa" in self.subl:
                for s in range(self.n_streams):
                    self.kv_prepass(s)
            ia, ib = 1, 0
            while ia < len(A) or ib < len(B):
                if ia < len(A) and ib < len(B):
                    ta, tb = A[ia][0], B[ib][0]
                    assert ta != tb
                    if ta == "L":
                        self.interleave((0, ia), A[ia][1](), (1, ib), B[ib][1]())
                    else:
                        self.interleave((1, ib), B[ib][1](), (0, ia), A[ia][1]())
                    ia += 1
                    ib += 1
                elif ia < len(A):
                    self.run_all((0, ia), A[ia][1]()); ia += 1
                else:
                    self.run_all((1, ib), B[ib][1]()); ib += 1
        for s in range(self.n_streams):
            sem_, cnt_ = self.st_sems[s], self.st_cnt[s]
            if cnt_:
                P.op("sp", lambda e, sem_=sem_, cnt_=cnt_: e.wait_ge(sem_, cnt_), [], [])


_CACHE = {}


def pack_weights(inp, builder):
    NPL = builder.NPL
    out = np.zeros((builder.NPIECES * 128, PIECE), np.float32)
    for l in range(DEPTH):
        for i, (nm, k0, nk, cols) in enumerate(builder.lspecs):
            idx = l * NPL + i
            out[idx * 128:(idx + 1) * 128] = pack_piece(inp[nm][l], k0, nk, cols)
        for i, (nm, k0, nk, cols) in enumerate(builder.kspecs):
            idx = DEPTH * NPL + l * 8 + i
            out[idx * 128:(idx + 1) * 128] = pack_piece(inp[nm][l], k0, nk, cols)
    return out


def make_in_maps(inp, builder, ncores=NCORES):
    inp = {k: np.asarray(v, dtype=np.float32) for k, v in inp.items()}
    wsrc = pack_weights(inp, builder)
    consts = make_constants()
    rope = make_rope_tables()
    params = np.ascontiguousarray(np.concatenate([make_params(inp, l) for l in range(DEPTH)], axis=1))
    wsmall = np.ascontiguousarray(np.concatenate([make_wsmall(inp, l) for l in range(DEPTH)], axis=1))
    maps = []
    for c in range(ncores):
        xs = inp["x"][c * SEQ_PER_CORE:(c + 1) * SEQ_PER_CORE].reshape(SEQ_PER_CORE * SEQ, D)
        xT = np.ascontiguousarray(xs.T)
        memT = np.ascontiguousarray(inp["mem"][c * SEQ_PER_CORE:(c + 1) * SEQ_PER_CORE].transpose(0, 2, 1))
        maps.append({"xT": xT, "memT": memT, "wsrc": wsrc, "consts": consts, "rope": rope,
                     "params": params, "wsmall": wsmall})
    return maps


def kernel(**inputs):
    if "builder" not in _CACHE:
        b = Builder()
        _CACHE["builder"] = b
        _CACHE["nc"] = b.build()
    b = _CACHE["builder"]
    nc = _CACHE["nc"]
    maps = make_in_maps(inputs, b)
    res = run_bass_kernel_spmd(nc, maps, core_ids=list(range(NCORES)))
    outs = []
    for c in range(NCORES):
        oT = np.asarray(res.results[c]["outT"])
        outs.append(oT.T.reshape(SEQ_PER_CORE, SEQ, D))
    return np.ascontiguousarray(np.concatenate(outs, axis=0).astype(np.float32))
```

```python
import numpy as np
import concourse.bass as bass
import concourse.mybir as mybir
from concourse.bass_utils import run_bass_kernel_spmd

F32 = mybir.dt.float32
BF16 = mybir.dt.bfloat16
ALU = mybir.AluOpType
AF = mybir.ActivationFunctionType

D = 1024
KC = 8
SEQ = 4096
DEPTH = 4
DFF = 2816
NF = DFF // 128
T = 512
NMEM = 256
EPS = 1e-6
NCORES = 8
SEQ_PER_CORE = 2
TILES_PER_SEQ = SEQ // T
PIECE = 2048
NSLOTS = 9
LOOKAHEAD = 6

O_RQ, O_RK, O_RV, O_RG = 0, 256, 512, 768
O_LX, O_LG = 1024, 1280
O_GQ, O_GK, O_GV, O_GA, O_GG = 1536, 1664, 1792, 2048, 2064
O_HQ, O_HF, O_HI, O_HG = 2320, 2576, 2832, 3088

P_FFN1, P_MIX, P_XA, P_MEM, P_FFN2 = 0, 8, 16, 24, 32
P_CONVW, P_CONVB, P_BA, P_BX, P_LAM, P_GBA, P_LB, P_FIN = 40, 48, 50, 52, 54, 56, 58, 60
NPAR = 68

C_AVGD = 0
C_BD64 = 128
C_ONES = 256
C_IDENT = 384
C_RESET = 512
C_DT = 1024
C_ML = 1280
C_MU = 1344
C_G1 = 1408
C_G2 = 1536
C_DEC = 1664
NCON = 1672


def _rot_cols(base):
    cols = []
    for h in range(4):
        for i in range(64):
            cols.append(base + h * 64 + (i + 32) % 64)
    return np.array(cols)


def _gla_pad_cols(base):
    cols = []
    for h in range(4):
        for i in range(64):
            cols.append(base + h * 32 + i if i < 32 else -1)
    return np.array(cols)


def _rng(a, n):
    return np.arange(a, a + n)


def layer_piece_specs():
    S = []

    def ffn(tag):
        for f in range(NF):
            cols = np.concatenate([_rng(f * 128, 128), _rng(DFF + f * 128, 128)])
            S.append((tag + "_w_gu", 0, 8, cols))
        for cp in range(4):
            for (k0, nk) in ((0, 8), (8, 8), (16, 6)):
                S.append((tag + "_w_down", k0, nk, _rng(cp * 256, 256)))

    ffn("ffn1")
    for cols in (_rng(O_RQ, 256), _rot_cols(O_RQ), _rng(O_RK, 256), _rot_cols(O_RK),
                 _rng(O_RV, 256), _rng(O_RG, 256),
                 _rng(O_LX, 256), _rng(O_LG, 256),
                 _gla_pad_cols(O_GQ), _gla_pad_cols(O_GK), _rng(O_GV, 256),
                 np.concatenate([_rng(O_GA, 16), -np.ones(240, dtype=np.int64)]), _rng(O_GG, 256),
                 _rng(O_HQ, 256), _rng(O_HF, 256), _rng(O_HI, 256), _rng(O_HG, 256)):
        S.append(("w_in", 0, 8, cols))
    for cp in range(4):
        S.append(("w_out", 0, 8, _rng(cp * 256, 256)))
    for cp in range(4):
        S.append(("xattn_wq", 0, 8, _rng(cp * 256, 256)))
    for cp in range(4):
        S.append(("xattn_wo", 0, 8, _rng(cp * 256, 256)))
    ffn("ffn2")
    return S


def kv_piece_specs():
    return [("xattn_wkv", 0, 8, _rng(cp * 256, 256)) for cp in range(8)]


def pack_piece(W, k0, nk, cols):
    out = np.zeros((128, PIECE), np.float32)
    valid = cols >= 0
    sub = np.zeros((nk * 128, 256), np.float32)
    sub[:, valid] = W[k0 * 128:(k0 + nk) * 128][:, cols[valid]]
    out[:, :nk * 256] = sub.reshape(nk, 128, 256).transpose(1, 0, 2).reshape(128, nk * 256)
    return out


def make_constants():
    c = np.zeros((128, NCON), np.float32)
    c[:, C_AVGD:C_AVGD + 128] = 1.0 / 1024.0
    bd = np.zeros((128, 128), np.float32)
    bd[:64, :64] = 1.0 / 64
    bd[64:, 64:] = 1.0 / 64
    c[:, C_BD64:C_BD64 + 128] = bd
    c[:, C_ONES:C_ONES + 128] = 1.0
    c[:, C_IDENT:C_IDENT + 128] = np.eye(128, dtype=np.float32)
    rm = np.ones((128, 512), np.float32)
    rm[:, ::64] = 0.0
    c[:, C_RESET:C_RESET + 512] = rm
    j = np.arange(64)
    gam = 1.0 - 2.0 ** (-5.0 - np.arange(4, dtype=np.float64))
    for h in range(4):
        Dm = gam[h] ** np.abs(j[:, None] - j[None, :]) * 0.125
        c[:64, C_DT + h * 64:C_DT + (h + 1) * 64] = Dm
    c[:64, C_ML:C_ML + 64] = (j[:, None] <= j[None, :]).astype(np.float32)
    c[:64, C_MU:C_MU + 64] = (j[:, None] > j[None, :]).astype(np.float32)
    for hp in range(2):
        for ph in range(2):
            h = hp * 2 + ph
            c[ph * 64:(ph + 1) * 64, C_G1 + hp * 64:C_G1 + (hp + 1) * 64] = (gam[h] ** (j + 1.0))[None, :]
            c[ph * 64:(ph + 1) * 64, C_G2 + hp * 64:C_G2 + (hp + 1) * 64] = (gam[h] ** (63.0 - j) * 0.125)[None, :]
            c[ph * 64:(ph + 1) * 64, C_DEC + hp] = gam[h] ** 64.0
    return c


def make_rope_tables():
    inv_freq = (np.float32(10000.0) ** (-np.arange(32, dtype=np.float32) / np.float32(32))).astype(np.float32)
    ang = (np.arange(SEQ, dtype=np.float32)[:, None] * inv_freq[None, :]).astype(np.float32)
    cos = np.cos(ang.astype(np.float64)).astype(np.float32).T
    sin = np.sin(ang.astype(np.float64)).astype(np.float32).T
    ct = np.concatenate([cos, cos, cos, cos], axis=0)
    st = np.concatenate([-sin, sin, -sin, sin], axis=0)
    return np.ascontiguousarray(np.stack([ct, st], axis=0))


def make_params(inp, l):
    p = np.zeros((128, NPAR), np.float32)

    def col8(v):
        return v.reshape(8, 128).T

    p[:, P_FFN1:P_FFN1 + 8] = col8(inp["ffn1_norm"][l])
    p[:, P_MIX:P_MIX + 8] = col8(inp["mix_norm"][l])
    p[:, P_XA:P_XA + 8] = col8(inp["xattn_norm"][l])
    p[:, P_MEM:P_MEM + 8] = col8(inp["mem_norm"][l])
    p[:, P_FFN2:P_FFN2 + 8] = col8(inp["ffn2_norm"][l])
    p[:, P_FIN:P_FIN + 8] = col8(inp["final_norm"])
    cw = inp["lru_conv_w"][l]
    for ch in range(2):
        for tap in range(4):
            p[:, P_CONVW + ch * 4 + tap] = cw[tap, ch * 128:(ch + 1) * 128]
    for name, off in (("lru_conv_b", P_CONVB), ("lru_ba", P_BA), ("lru_bx", P_BX), ("lru_lambda", P_LAM)):
        p[:, off:off + 2] = inp[name][l].reshape(2, 128).T
    ba = inp["gla_b_a"][l]
    bpad = np.zeros(256, np.float32)
    for h in range(4):
        bpad[h * 64:h * 64 + 32] = ba[h * 32:(h + 1) * 32]
    p[:, P_GBA:P_GBA + 2] = bpad.reshape(2, 128).T
    p[:, P_LB:P_LB + 2] = inp["hgrn_lb_logits"][l].reshape(2, 128).T
    return p


NWS = 768


def make_wsmall(inp, l):
    w = np.zeros((128, NWS), np.float32)
    for nm, off in (("lru_wa", 0), ("lru_wx", 256)):
        W = inp[nm][l]
        for ch in range(2):
            for b in range(2):
                n = ch * 2 + b
                w[b * 64:(b + 1) * 64, off + ch * 128 + b * 64: off + ch * 128 + (b + 1) * 64] = W[n]
    wa2 = inp["gla_w_a2"][l]
    for h in range(4):
        w[:16, 512 + h * 64: 512 + h * 64 + 32] = wa2[:, h * 32:(h + 1) * 32]
    return w


class TT:
    __slots__ = ("name", "last_w", "readers")

    def __init__(self, name):
        self.name = name
        self.last_w = None
        self.readers = {}


class _Ret:
    pass


class _Rec:
    def __init__(self):
        self.call = None

    def __getattr__(self, name):
        def f(*a, **k):
            assert self.call is None
            self.call = (name, a, k)
            return _Ret()
        return f


class Op:
    __slots__ = ("eng", "fn", "deps", "signal", "count", "token", "idx")

    def __init__(self, eng, fn):
        self.eng = eng
        self.fn = fn
        self.deps = []
        self.signal = False
        self.count = 0
        self.token = None
        self.idx = 0


COMPUTE = ("pe", "act", "dve", "pool")
ALLENG = ("pe", "act", "dve", "pool", "sp")


class Prog:
    def __init__(self):
        self.q = {e: [] for e in ALLENG}
        self.nops = 0

    def op(self, eng, fn, reads=(), writes=(), dma=None):
        rec = _Rec()
        fn(rec)
        name, a, k = rec.call
        o = Op(eng, (name, a, k))
        o.idx = self.nops
        self.nops += 1
        if dma is not None:
            o.token = dma
        deps = {}
        for t in reads:
            if t.last_w is not None:
                deps[id(t.last_w)] = t.last_w
        for t in writes:
            if t.last_w is not None:
                deps[id(t.last_w)] = t.last_w
            for r in t.readers.values():
                deps[id(r)] = r
        for d in deps.values():
            if d is o:
                continue
            if d.token is None and d.eng == eng == "pe":
                continue
            o.deps.append(d)
            d.signal = True
        for t in reads:
            key = eng if dma is None else ("dma", o.idx)
            t.readers[key] = o
        for t in writes:
            t.last_w = o
            t.readers = {}
        self.q[eng].append(o)
        return o

    def emit(self, nc, block, sems):
        for e in COMPUTE:
            c = 0
            for o in self.q[e]:
                if o.token is None and o.signal:
                    c += 1
                    o.count = c
        q = self.q

        def run(engname, e):
            known = {}
            for o in q[engname]:
                need = {}
                for d in o.deps:
                    if d.token is not None:
                        s, v = d.token
                    else:
                        s, v = sems[d.eng], d.count
                    k = id(s)
                    if known.get(k, 0) >= v:
                        continue
                    if k not in need or need[k][1] < v:
                        need[k] = (s, v)
                for k, (s, v) in need.items():
                    e.wait_ge(s, v)
                    known[k] = v
                name, a, k = o.fn
                ins = getattr(e, name)(*a, **k)
                if o.token is not None:
                    ins.then_inc(o.token[0], 16)
                elif o.signal:
                    ins.then_inc(sems[engname], 1)

        @block.tensor
        def _(e):
            run("pe", e)

        @block.scalar
        def _(e):
            run("act", e)

        @block.vector
        def _(e):
            run("dve", e)

        @block.gpsimd
        def _(e):
            run("pool", e)

        @block.sync
        def _(e):
            run("sp", e)


class DryProg:
    def __init__(self):
        self.nops = 0
        self.q = {e: [] for e in ALLENG}

    def op(self, eng, fn, reads=(), writes=(), dma=None):
        self.nops += 1
        return None


class V:
    __slots__ = ("ap", "tts", "slot")

    def __init__(self, ap, tts, slot=None):
        self.ap = ap
        self.tts = list(tts)
        self.slot = slot


class Arena:
    def __init__(self, tensor, ng, name):
        self.tensor = tensor
        self.ng = ng
        self.ptr = 0
        self.tts = [TT("%s%d" % (name, i)) for i in range(ng)]

    def alloc(self, nbytes_per_part, dt):
        ng = (nbytes_per_part + 1023) // 1024
        assert self.ptr + ng <= self.ng, "arena overflow %d+%d>%d" % (self.ptr, ng, self.ng)
        g0 = self.ptr
        self.ptr += ng
        ap = self.tensor[:, g0 * 512:(g0 + ng) * 512]
        if dt == F32:
            ap = ap.bitcast(F32)[:, 0:nbytes_per_part // 4]
        else:
            ap = ap[:, 0:nbytes_per_part // 2]
        return V(ap, self.tts[g0:g0 + ng])


    def sub(self, ng):
        assert self.ptr + ng <= self.ng, "arena overflow (sub) %d+%d>%d" % (self.ptr, ng, self.ng)
        c = Arena.__new__(Arena)
        c.tensor = self.tensor[:, self.ptr * 512:(self.ptr + ng) * 512]
        c.ng = ng
        c.ptr = 0
        c.tts = self.tts[self.ptr:self.ptr + ng]
        self.ptr += ng
        return c


def par(gens):
    gens = list(gens)
    while gens:
        for g in list(gens):
            try:
                yield next(g)
            except StopIteration:
                gens.remove(g)


class Stream:
    pass


N_FFN = NF + 12
NKV = 2 * DEPTH * 2


class Builder:
    def __init__(self, tiles_per_seq=TILES_PER_SEQ, n_layers=DEPTH, subl=("ffn1", "mix", "xa", "ffn2"),
                 mixers=("ret", "lru", "gla", "hgrn"), final_norm=True, n_streams=2):
        self.tps = tiles_per_seq
        self.n_layers = n_layers
        self.subl = subl
        self.mixers = mixers
        self.final_norm = final_norm
        self.n_streams = n_streams
        self.lspecs = layer_piece_specs()
        self.kspecs = kv_piece_specs()
        self.NPL = len(self.lspecs)
        self.NPIECES = DEPTH * self.NPL + DEPTH * 8
        self.KV0 = self.NPIECES

    def build(self):
        nc = bass.Bass("TRN2", target_bir_lowering=False)
        self.nc = nc
        dr = {}
        dr["xT"] = nc.dram_tensor("xT", [D, SEQ_PER_CORE * SEQ], F32, kind="ExternalInput").ap()
        dr["memT"] = nc.dram_tensor("memT", [SEQ_PER_CORE, D, NMEM], F32, kind="ExternalInput").ap()
        dr["wsrc"] = nc.dram_tensor("wsrc", [self.NPIECES * 128, PIECE], F32, kind="ExternalInput").ap()
        dr["consts"] = nc.dram_tensor("consts", [128, NCON], F32, kind="ExternalInput").ap()
        dr["rope"] = nc.dram_tensor("rope", [2, 128, SEQ], F32, kind="ExternalInput").ap()
        dr["params"] = nc.dram_tensor("params", [128, DEPTH * NPAR], F32, kind="ExternalInput").ap()
        dr["wsmall"] = nc.dram_tensor("wsmall", [128, DEPTH * NWS], F32, kind="ExternalInput").ap()
        dr["outT"] = nc.dram_tensor("outT", [D, SEQ_PER_CORE * SEQ], F32, kind="ExternalOutput").ap()
        dr["wbf"] = nc.dram_tensor("wbf", [(self.NPIECES + NKV) * 128, PIECE], BF16, kind="Internal").ap()
        self.dr = dr

        import contextlib
        with contextlib.ExitStack() as es:
            def sb(name, shape, dt):
                return es.enter_context(nc.sbuf_tensor(name, shape, dt))

            def sem(name):
                return es.enter_context(nc.semaphore(name))

            self.sems = {e: sem("s_" + e) for e in COMPUTE}
            self.slot_sems = [sem("slot%d" % i) for i in range(NSLOTS)]
            self.NREG = DEPTH * 3 + 1
            self.cast_sems = [sem("cast%d" % i) for i in range(self.NREG)]
            self.st_sems = [sem("st%d" % i) for i in range(2)]
            self.misc_sems = [sem("misc%d" % i) for i in range(8)]
            self.kvst_sem = sem("kvst")
            self.cf = sb("cf", [128, NCON], F32)
            self.cb = sb("cb", [128, 512], BF16)
            self.par = sb("par", [128, DEPTH * NPAR], F32)
            self.der = sb("der", [128, DEPTH * 16], F32)
            self.wsm = sb("wsm", [128, DEPTH * NWS], BF16)
            self.epsc = sb("epsc", [128, 2], F32)
            self.slots = sb("slots", [128, NSLOTS, PIECE], BF16)
            self.sx = [sb("x%d" % s, [128, KC, T], F32) for s in range(2)]
            self.sxn = [sb("xn%d" % s, [128, KC, T], BF16) for s in range(2)]
            self.scs = [sb("cs%d" % s, [128, 2, T], F32) for s in range(2)]
            self.sS = [sb("S%d" % s, [128, DEPTH * 6, 128], F32) for s in range(2)]
            self.shst = [sb("hst%d" % s, [128, DEPTH * 2], F32) for s in range(2)]
            self.shalo = [sb("halo%d" % s, [128, DEPTH * 2, 4], F32) for s in range(2)]
            NG_M, NG_F = 48, 26
            self.wm = sb("work_m", [128, NG_M * 512], BF16)
            self.wf = sb("work_f", [128, NG_F * 512], BF16)
            self.NG_M, self.NG_F = NG_M, NG_F
            self.ps = [es.enter_context(nc.psum_tensor("ps%d" % i, [128, 512], F32)) for i in range(8)]

            self.order = None
            self.unit_counts = None
            self.reset_state(DryProg())
            self.dry = True
            self.emit_all()
            self.unit_counts = self.rec_units
            self.reset_state(DryProg())
            self.emit_all()
            self.order = self.rec_order
            self.P = Prog()
            self.reset_state(self.P)
            self.dry = False
            self.emit_all()
            assert self.s_next == len(self.order)
            with nc.Block() as block:
                self.P.emit(nc, block, self.sems)
        return nc

    def reset_state(self, P):
        self.P = P
        self.misc_cnt = [0] * 8
        self.st_cnt = [0] * 2
        self.kvst_cnt = 0
        self.t_cf = TT("cf"); self.t_cb = TT("cb"); self.t_par = TT("par"); self.t_der = TT("der")
        self.t_wsm = TT("wsm"); self.t_epsc = TT("epsc")
        self.t_misc = [TT("misc%d" % i) for i in range(8)]
        self.t_slots = [TT("slot%d" % i) for i in range(NSLOTS)]
        self.t_wbf = [TT("wbf%d" % i) for i in range(self.NREG)]
        self.t_kvd = [TT("kvd%d" % i) for i in range(2)]
        self.t_ps = [TT("ps%d" % i) for i in range(8)]
        self.ar_m = Arena(self.wm, self.NG_M, "wm")
        self.ar_f = Arena(self.wf, self.NG_F, "wf")
        self.bank_rr = {"F": 0, "M": 0}
        self.streams = []
        for s in range(2):
            st = Stream()
            st.s = s
            st.x, st.xn, st.cs, st.S, st.hst, st.halo = self.sx[s], self.sxn[s], self.scs[s], self.sS[s], self.shst[s], self.shalo[s]
            st.t_x = [TT("x%d_%d" % (s, i)) for i in range(KC)]
            st.t_xn = [TT("xn%d_%d" % (s, i)) for i in range(KC)]
            st.t_cs = TT("cs%d" % s)
            st.t_S = [TT("S%d_%d" % (s, i)) for i in range(DEPTH * 6)]
            st.t_hst = [TT("hst%d_%d" % (s, i)) for i in range(DEPTH)]
            st.t_halo = [TT("halo%d_%d" % (s, i)) for i in range(DEPTH)]
            self.streams.append(st)
        self.rec_order = []
        self.rec_units = {}
        self.s_next = 0
        self.s_issued = 0
        self.free_slots = list(range(NSLOTS))
        self.slot_of = {}
        self.slot_cnt = [0] * NSLOTS

    BANKS = {"F": (0, 1, 2, 3), "M": (4, 5, 6, 7)}

    def bank(self, pool):
        i = self.bank_rr[pool]
        self.bank_rr[pool] = (i + 1) % 4
        return self.BANKS[pool][i]

    def mm(self, out, lhsT, rhs, start, stop, reads, writes, **kw):
        self.P.op("pe", lambda e: e.matmul(out, lhsT, rhs, start=start, stop=stop, **kw), reads, writes)

    def pcol(self, l, col):
        return self.par[:, l * NPAR + col: l * NPAR + col + 1]

    def dcol(self, l, col):
        return self.der[:, l * 16 + col: l * 16 + col + 1]

    def rsqrt_eps(self, out, in_, reads, writes):
        self.P.op("act", lambda e: e.activation(out=out, in_=in_, func=AF.Ln, bias=self.epsc[:, 0:1]),
                  list(reads) + [self.t_epsc], writes)
        self.P.op("act", lambda e: e.activation(out=out, in_=out, func=AF.Exp, scale=-0.5), writes, writes)

    def dma(self, eng, out, in_, reads, writes, semk):
        self.misc_cnt[semk] += 16
        tok = (self.misc_sems[semk], self.misc_cnt[semk])
        return self.P.op(eng, lambda e: e.dma_start(out=out, in_=in_), reads, list(writes) + [self.t_misc[semk]], dma=tok)

    def region_of(self, key):
        if key[0] == "L":
            _, l, i = key
            grp = 0 if i < N_FFN else (2 if i >= self.NPL - N_FFN else 1)
            return l * 3 + grp
        if key[0] == "kv":
            return DEPTH * 3
        return None

    def pidx_of(self, key):
        if key[0] == "L":
            return key[1] * self.NPL + key[2]
        if key[0] == "kv":
            return DEPTH * self.NPL + key[1] * 8 + key[2]
        _, seq, l, j = key
        return self.KV0 + (seq * DEPTH + l) * 2 + j

    def _issue(self, s):
        key = self.order[s]
        slot = self.free_slots.pop(0)
        self.slot_of[s] = slot
        self.slot_cnt[slot] += 16
        tok = (self.slot_sems[slot], self.slot_cnt[slot])
        pidx = self.pidx_of(key)
        dst = self.slots[:, slot, :]
        src = self.dr["wbf"][pidx * 128:(pidx + 1) * 128, :]
        rd = [self.t_wbf[self.region_of(key)]] if key[0] != "kvs" else [self.t_kvd[key[1]]]
        self.P.op("sp", lambda e: e.dma_start(out=dst, in_=src), rd, [self.t_slots[slot]], dma=tok)

    def _pump(self):
        if self.dry:
            return
        while (self.s_issued < len(self.order) and self.free_slots and self.s_issued < self.s_next + LOOKAHEAD):
            self._issue(self.s_issued)
            self.s_issued += 1

    def wnext(self, key):
        if self.dry:
            self.rec_order.append(key)
            return V(self.slots[:, 0, :], [self.t_slots[0]], slot=None)
        s = self.s_next
        assert self.order[s] == key, (self.order[s], key)
        self.s_next += 1
        if self.s_issued <= s:
            assert self.free_slots, "weight ring exhausted (too many pieces held)"
        self._pump()
        assert self.s_issued > s
        slot = self.slot_of.pop(s)
        return V(self.slots[:, slot, :], [self.t_slots[slot]], slot=slot)

    def wrel(self, w):
        if self.dry:
            return
        self.free_slots.append(w.slot)
        self._pump()

    def cast_regions(self, regions):
        P = self.P
        dr = self.dr
        CH = 4
        for region in regions:
            if region < DEPTH * 3:
                l, grp = region // 3, region % 3
                a = (0, N_FFN, self.NPL - N_FFN)[grp]
                b = (N_FFN, self.NPL - N_FFN, self.NPL)[grp]
                p0, p1 = l * self.NPL + a, l * self.NPL + b
            else:
                p0, p1 = DEPTH * self.NPL, self.NPIECES
            cnt = 0
            i = p0
            while i < p1:
                n = min(CH, p1 - i)
                cnt += 16
                tok = (self.cast_sems[region], cnt)
                dst = dr["wbf"][i * 128:(i + n) * 128, :]
                src = dr["wsrc"][i * 128:(i + n) * 128, :]
                P.op("pool", lambda e, dst=dst, src=src: e.dma_start(out=dst, in_=src), [], [self.t_wbf[region]], dma=tok)
                i += n

    def prologue(self):
        P = self.P
        dr = self.dr
        self.dma("sp", self.cf[:, :], dr["consts"][:, :], [], [self.t_cf], 0)
        self.dma("sp", self.par[:, :], dr["params"][:, :], [], [self.t_par], 1)
        P.op("pool", lambda e: e.memset(self.epsc[:, 0:1], EPS), [], [self.t_epsc])
        P.op("pool", lambda e: e.memset(self.epsc[:, 1:2], 1.0), [], [self.t_epsc])
        P.op("dve", lambda e: e.tensor_copy(out=self.cb[:, :], in_=self.cf[:, 0:512]), [self.t_cf], [self.t_cb])
        for st in self.streams:
            P.op("pool", lambda e, st=st: e.memset(st.S[:, :, :], 0.0), [], st.t_S)
            P.op("pool", lambda e, st=st: e.memset(st.hst[:, :], 0.0), [], st.t_hst)
            P.op("pool", lambda e, st=st: e.memset(st.halo[:, :, :], 0.0), [], st.t_halo)
        self.cast_regions([0, DEPTH * 3, 1, 2])
        ar = self.ar_m
        ar.ptr = 0
        for l in range(DEPTH):
            stg = ar.alloc(NWS * 4, F32)
            self.dma("sp", stg.ap, dr["wsmall"][:, l * NWS:(l + 1) * NWS], [], stg.tts, 2)
            dst = self.wsm[:, l * NWS:(l + 1) * NWS]
            P.op("dve", lambda e, dst=dst, src=stg.ap: e.tensor_copy(out=dst, in_=src), stg.tts, [self.t_wsm])
        tmp = ar.alloc(64 * 4, F32)
        ta = tmp.ap
        rd = [self.t_par] + tmp.tts
        wr = [self.t_der] + tmp.tts
        for l in range(DEPTH):
            src = self.par[:, l * NPAR + P_LB: l * NPAR + P_LB + 2]
            P.op("act", lambda e, o=ta[:, l * 2:l * 2 + 2], s=src: e.activation(out=o, in_=s, func=AF.Exp), rd, wr)
        P.op("dve", lambda e: e.tensor_add(out=ta[:, 8:10], in0=ta[:, 0:2], in1=ta[:, 2:4]), rd, wr)
        P.op("dve", lambda e: e.tensor_add(out=ta[:, 8:10], in0=ta[:, 8:10], in1=ta[:, 4:6]), rd, wr)
        P.op("dve", lambda e: e.tensor_add(out=ta[:, 8:10], in0=ta[:, 8:10], in1=ta[:, 6:8]), rd, wr)
        P.op("dve", lambda e: e.reciprocal(out=ta[:, 10:12], in_=ta[:, 8:10]), rd, wr)
        for l in range(DEPTH):
            P.op("dve", lambda e, l=l: e.tensor_mul(out=ta[:, 12 + 2 * l:14 + 2 * l], in0=ta[:, 2 * l:2 * l + 2],
                                                    in1=ta[:, 10:12]), rd, wr)
        P.op("dve", lambda e: e.memset(self.der[:, 0:2], 0.0), rd, wr)
        for l in range(1, DEPTH):
            P.op("dve", lambda e, l=l: e.tensor_add(out=self.der[:, l * 16:l * 16 + 2],
                                                    in0=self.der[:, (l - 1) * 16:(l - 1) * 16 + 2],
                                                    in1=ta[:, 12 + 2 * l:14 + 2 * l]), rd, wr)
        for l in range(DEPTH):
            b = l * 16
            P.op("dve", lambda e, b=b: e.tensor_scalar(out=self.der[:, b + 2:b + 4], in0=self.der[:, b:b + 2],
                                                       scalar1=-1.0, scalar2=1.0, op0=ALU.mult, op1=ALU.add), rd, wr)
            lam = self.par[:, l * NPAR + P_LAM: l * NPAR + P_LAM + 2]
            P.op("act", lambda e, l=l, lam=lam: e.activation(out=ta[:, 20 + 2 * l:22 + 2 * l], in_=lam, func=AF.Exp,
                                                             scale=-1.0), rd, wr)
            P.op("act", lambda e, l=l: e.activation(out=ta[:, 28 + 2 * l:30 + 2 * l], in_=ta[:, 20 + 2 * l:22 + 2 * l],
                                                    func=AF.Ln, bias=self.epsc[:, 1:2]), rd + [self.t_epsc], wr)
            P.op("dve", lambda e, l=l, b=b: e.tensor_scalar(out=self.der[:, b + 4:b + 6], in0=ta[:, 28 + 2 * l:30 + 2 * l],
                                                            scalar1=-8.0, scalar2=None, op0=ALU.mult), rd, wr)
            P.op("dve", lambda e, l=l, b=b: e.tensor_scalar(out=self.der[:, b + 6:b + 8], in0=ta[:, 28 + 2 * l:30 + 2 * l],
                                                            scalar1=-16.0, scalar2=None, op0=ALU.mult), rd, wr)
            gba = self.par[:, l * NPAR + P_GBA: l * NPAR + P_GBA + 2]
            P.op("dve", lambda e, b=b, gba=gba: e.tensor_scalar(out=self.der[:, b + 8:b + 10], in0=gba,
                                                                scalar1=-1.0, scalar2=None, op0=ALU.mult), rd, wr)

    def rmsnorm(self, st, l, gcol, ar, pool):
        P = self.P
        sq = ar.alloc(KC * T * 2, BF16)
        rstd = ar.alloc(T * 4, F32)
        for q4 in range(4):
            src = st.x[:, q4 * 2:(q4 + 1) * 2, :]
            dst = sq.ap[:, q4 * 2 * T:(q4 + 1) * 2 * T].rearrange("p (a b) -> p a b", b=T)
            P.op("act", lambda e, src=src, dst=dst: e.activation(out=dst, in_=src, func=AF.Square),
                 st.t_x[q4 * 2:(q4 + 1) * 2], sq.tts[q4 * 2:(q4 + 1) * 2])
        b = self.bank(pool)
        for kc in range(KC):
            self.mm(self.ps[b][:, :], self.cb[:, C_AVGD:C_AVGD + 128], sq.ap[:, kc * T:(kc + 1) * T],
                    kc == 0, kc == KC - 1, [self.t_cb, sq.tts[kc]], [self.t_ps[b]])
        self.rsqrt_eps(rstd.ap, self.ps[b][:, :], [self.t_ps[b]], rstd.tts)
        for kc in range(KC):
            g = self.pcol(l, gcol + kc)
            P.op("dve", lambda e, kc=kc, g=g: e.scalar_tensor_tensor(out=st.xn[:, kc, :], in0=st.x[:, kc, :], scalar=g,
                                                                    in1=rstd.ap, op0=ALU.mult, op1=ALU.mult),
                 [st.t_x[kc], self.t_par] + rstd.tts, [st.t_xn[kc]])

    def ffn(self, st, l, gcol, base_i):
        P = self.P
        ar = self.ar_f
        ar.ptr = 0
        if st.s == 0 and st.cur_k == 0 and gcol == P_FFN1 and l + 1 < DEPTH:
            self.cast_regions([(l + 1) * 3, (l + 1) * 3 + 1, (l + 1) * 3 + 2])
        h = ar.alloc(NF * T * 2, BF16)
        sgs = [ar.alloc(T * 2, BF16) for _ in range(2)]
        save = ar.ptr
        ar.ptr = 12
        self.rmsnorm(st, l, gcol, ar, "F")
        ar.ptr = save
        yield
        i = base_i
        for f in range(NF):
            w = self.wnext(("L", l, i)); i += 1
            bg, bu = self.bank("F"), self.bank("F")
            for j, b in ((0, bg), (1, bu)):
                for kc in range(KC):
                    self.mm(self.ps[b][:, :], w.ap[:, kc * 256 + j * 128: kc * 256 + (j + 1) * 128], st.xn[:, kc, :],
                            kc == 0, kc == KC - 1, w.tts + [st.t_xn[kc]], [self.t_ps[b]])
                if j == 0:
                    yield
            self.wrel(w)
            sg = sgs[f % 2]
            P.op("act", lambda e, sg=sg, bg=bg: e.activation(out=sg.ap, in_=self.ps[bg][:, :], func=AF.Silu),
                 [self.t_ps[bg]], sg.tts)
            P.op("dve", lambda e, sg=sg, bu=bu, f=f: e.tensor_tensor(out=h.ap[:, f * T:(f + 1) * T], in0=sg.ap,
                                                                   in1=self.ps[bu][:, :], op=ALU.mult),
                 [self.t_ps[bu]] + sg.tts, [h.tts[f]])
            yield
        for cp in range(4):
            b2 = [self.bank("F"), self.bank("F")]
            for (k0, nk) in ((0, 8), (8, 8), (16, 6)):
                w = self.wnext(("L", l, i)); i += 1
                for j in range(2):
                    for fk in range(k0, k0 + nk):
                        self.mm(self.ps[b2[j]][:, :], w.ap[:, (fk - k0) * 256 + j * 128:(fk - k0) * 256 + (j + 1) * 128],
                                h.ap[:, fk * T:(fk + 1) * T], fk == 0, fk == NF - 1,
                                w.tts + [h.tts[fk]], [self.t_ps[b2[j]]])
                    if j == 0:
                        yield
                self.wrel(w)
                yield
            for j in range(2):
                dc = cp * 2 + j
                P.op("dve", lambda e, dc=dc, b=b2[j]: e.scalar_tensor_tensor(out=st.x[:, dc, :], in0=self.ps[b][:, :],
                                                                             scalar=0.5, in1=st.x[:, dc, :],
                                                                             op0=ALU.mult, op1=ALU.add),
                     [self.t_ps[b2[j]], st.t_x[dc]], [st.t_x[dc]])
        yield

    def kv_prepass(self, seq):
        P = self.P
        ar = self.ar_m
        ar.ptr = 0
        memT = ar.alloc(KC * NMEM * 4, F32)
        msq = ar.alloc(KC * NMEM * 2, BF16)
        mrstd = ar.alloc(NMEM * 4, F32)
        memn = ar.alloc(KC * NMEM * 2, BF16)
        stg = [ar.alloc(PIECE * 2, BF16) for _ in range(4)]
        src = self.dr["memT"][seq].rearrange("(kc p) n -> p kc n", p=128)
        dst = memT.ap.rearrange("p (a b) -> p a b", b=NMEM)
        self.dma("sp", dst, src, [], memT.tts, 3)
        P.op("act", lambda e: e.activation(out=msq.ap, in_=memT.ap, func=AF.Square), memT.tts, msq.tts)
        b = self.bank("M")
        for kc in range(KC):
            self.mm(self.ps[b][:, 0:NMEM], self.cb[:, C_AVGD:C_AVGD + 128], msq.ap[:, kc * NMEM:(kc + 1) * NMEM],
                    kc == 0, kc == KC - 1, [self.t_cb] + msq.tts, [self.t_ps[b]])
        self.rsqrt_eps(mrstd.ap, self.ps[b][:, 0:NMEM], [self.t_ps[b]], mrstd.tts)
        for l in range(self.n_layers):
            for kc in range(KC):
                g = self.pcol(l, P_MEM + kc)
                P.op("dve", lambda e, kc=kc, g=g: e.scalar_tensor_tensor(
                    out=memn.ap[:, kc * NMEM:(kc + 1) * NMEM], in0=memT.ap[:, kc * NMEM:(kc + 1) * NMEM], scalar=g,
                    in1=mrstd.ap, op0=ALU.mult, op1=ALU.mult), memT.tts + mrstd.tts + [self.t_par], memn.tts)
            sk = stg[(l % 2) * 2]
            sv = stg[(l % 2) * 2 + 1]
            for cp in range(8):
                w = self.wnext(("kv", l, cp))
                if cp < 4:
                    for j in range(2):
                        dc = cp * 2 + j
                        b = self.bank("M")
                        for kc in range(KC):
                            self.mm(self.ps[b][:, 0:NMEM], w.ap[:, kc * 256 + j * 128: kc * 256 + (j + 1) * 128],
                                    memn.ap[:, kc * NMEM:(kc + 1) * NMEM], kc == 0, kc == KC - 1,
                                    w.tts + memn.tts, [self.t_ps[b]])
                        dstk = sk.ap[:, dc * NMEM:(dc + 1) * NMEM]
                        P.op("act", lambda e, dstk=dstk, b=b: e.activation(out=dstk, in_=self.ps[b][:, 0:NMEM], func=AF.Copy),
                             [self.t_ps[b]], sk.tts)
                else:
                    for mh in range(2):
                        b = self.bank("M")
                        for kc in range(KC):
                            self.mm(self.ps[b][:, 0:256], memn.ap[:, kc * NMEM + mh * 128: kc * NMEM + (mh + 1) * 128],
                                    w.ap[:, kc * 256:(kc + 1) * 256], kc == 0, kc == KC - 1,
                                    w.tts + memn.tts, [self.t_ps[b]])
                        dstv = sv.ap[:, mh * D + (cp - 4) * 256: mh * D + (cp - 3) * 256]
                        P.op("dve", lambda e, dstv=dstv, b=b: e.tensor_copy(out=dstv, in_=self.ps[b][:, 0:256]),
                             [self.t_ps[b]], sv.tts)
                self.wrel(w)
            for j, sg_ in ((0, sk), (1, sv)):
                pidx = self.pidx_of(("kvs", seq, l, j))
                self.kvst_cnt += 16
                tok = (self.kvst_sem, self.kvst_cnt)
                dstd = self.dr["wbf"][pidx * 128:(pidx + 1) * 128, :]
                P.op("act", lambda e, dstd=dstd, sg_=sg_: e.dma_start(out=dstd, in_=sg_.ap), sg_.tts,
                     [self.t_kvd[seq], self.t_misc[5]], dma=tok)

    def xattn(self, st, l, base_i):
        P = self.P
        ar = self.ar_m
        ar.ptr = 0
        self.rmsnorm(st, l, P_XA, ar, "M")
        ar.ptr = 0
        qT = ar.alloc(KC * T * 2, BF16)
        pT = ar.alloc(KC * T * 2, BF16)
        oT = ar.alloc(KC * T * 2, BF16)
        rden = [ar.alloc(T * 4, F32) for _ in range(4)]
        yield
        i = base_i
        for cp in range(4):
            w = self.wnext(("L", l, i)); i += 1
            for j in range(2):
                dc = cp * 2 + j
                b = self.bank("M")
                for kc in range(KC):
                    self.mm(self.ps[b][:, :], w.ap[:, kc * 256 + j * 128: kc * 256 + (j + 1) * 128], st.xn[:, kc, :],
                            kc == 0, kc == KC - 1, w.tts + [st.t_xn[kc]], [self.t_ps[b]])
                dst = qT.ap[:, dc * T:(dc + 1) * T]
                if j == 0:
                    P.op("act", lambda e, dst=dst, b=b: e.activation(out=dst, in_=self.ps[b][:, :], func=AF.Copy, scale=1.0 / 16),
                         [self.t_ps[b]], [qT.tts[dc]])
                else:
                    P.op("dve", lambda e, dst=dst, b=b: e.tensor_scalar(out=dst, in0=self.ps[b][:, :], scalar1=1.0 / 16,
                                                                        scalar2=None, op0=ALU.mult),
                         [self.t_ps[b]], [qT.tts[dc]])
            self.wrel(w)
            yield
        wk = self.wnext(("kvs", st.s, l, 0))
        wv = self.wnext(("kvs", st.s, l, 1))
        def head_task(h):
            for mh in range(2):
                b = self.bank("M")
                for d2 in range(2):
                    dc = h * 2 + d2
                    self.mm(self.ps[b][:, :], wk.ap[:, dc * NMEM + mh * 128: dc * NMEM + (mh + 1) * 128],
                            qT.ap[:, dc * T:(dc + 1) * T], d2 == 0, d2 == 1,
                            wk.tts + [qT.tts[dc]], [self.t_ps[b]])
                dst = pT.ap[:, (h * 2 + mh) * T:(h * 2 + mh + 1) * T]
                P.op("act", lambda e, dst=dst, b=b: e.activation(out=dst, in_=self.ps[b][:, :], func=AF.Exp),
                     [self.t_ps[b]], [pT.tts[h * 2 + mh]])
            yield
            bd = self.bank("M")
            rd_ = rden[h]
            for mh in range(2):
                self.mm(self.ps[bd][:, :], self.cb[:, C_ONES:C_ONES + 128], pT.ap[:, (h * 2 + mh) * T:(h * 2 + mh + 1) * T],
                        mh == 0, mh == 1, [self.t_cb, pT.tts[h * 2 + mh]], [self.t_ps[bd]])
            P.op("act", lambda e, bd=bd, rd_=rd_: e.activation(out=rd_.ap, in_=self.ps[bd][:, :], func=AF.Ln), [self.t_ps[bd]], rd_.tts)
            P.op("act", lambda e, rd_=rd_: e.activation(out=rd_.ap, in_=rd_.ap, func=AF.Exp, scale=-1.0), rd_.tts, rd_.tts)
            for ec in range(2):
                b = self.bank("M")
                for mh in range(2):
                    self.mm(self.ps[b][:, :], wv.ap[:, mh * D + h * 256 + ec * 128: mh * D + h * 256 + (ec + 1) * 128],
                            pT.ap[:, (h * 2 + mh) * T:(h * 2 + mh + 1) * T], mh == 0, mh == 1,
                            wv.tts + [pT.tts[h * 2 + mh]], [self.t_ps[b]])
                dst = oT.ap[:, (h * 2 + ec) * T:(h * 2 + ec + 1) * T]
                P.op("dve", lambda e, dst=dst, b=b, rd_=rd_: e.tensor_tensor(out=dst, in0=self.ps[b][:, :], in1=rd_.ap, op=ALU.mult),
                     [self.t_ps[b]] + rd_.tts, [oT.tts[h * 2 + ec]])
            yield
        yield from par([head_task(h) for h in range(4)])
        self.wrel(wk)
        self.wrel(wv)
        for cp in range(4):
            w = self.wnext(("L", l, i)); i += 1
            for j in range(2):
                dc = cp * 2 + j
                b = self.bank("M")
                for ec in range(KC):
                    self.mm(self.ps[b][:, :], w.ap[:, ec * 256 + j * 128: ec * 256 + (j + 1) * 128],
                            oT.ap[:, ec * T:(ec + 1) * T], ec == 0, ec == KC - 1, w.tts + [oT.tts[ec]], [self.t_ps[b]])
                P.op("dve", lambda e, dc=dc, b=b: e.tensor_tensor(out=st.x[:, dc, :], in0=self.ps[b][:, :],
                                                                  in1=st.x[:, dc, :], op=ALU.add),
                     [self.t_ps[b], st.t_x[dc]], [st.t_x[dc]])
            self.wrel(w)
            yield

    def projF(self, st, w, j):
        b = self.bank("M")
        for kc in range(KC):
            self.mm(self.ps[b][:, :], w.ap[:, kc * 256 + j * 128: kc * 256 + (j + 1) * 128], st.xn[:, kc, :],
                    kc == 0, kc == KC - 1, w.tts + [st.t_xn[kc]], [self.t_ps[b]])
        return b

    def projT(self, st, w, vt):
        P = self.P
        for c2 in range(4):
            b = self.bank("M")
            for cc in range(2):
                c = c2 * 2 + cc
                for kc in range(KC):
                    self.mm(self.ps[b][0:64, cc * 256:(cc + 1) * 256], st.xn[:, kc, c * 64:(c + 1) * 64],
                            w.ap[:, kc * 256:(kc + 1) * 256], kc == 0, kc == KC - 1,
                            w.tts + [st.t_xn[kc]], [self.t_ps[b]])
            dst = vt.ap[0:64, c2 * 512:(c2 + 1) * 512]
            if c2 % 2 == 0:
                P.op("act", lambda e, dst=dst, b=b: e.activation(out=dst, in_=self.ps[b][0:64, :], func=AF.Copy),
                     [self.t_ps[b]], vt.tts)
            else:
                P.op("dve", lambda e, dst=dst, b=b: e.tensor_copy(out=dst, in_=self.ps[b][0:64, :]),
                     [self.t_ps[b]], vt.tts)
            yield

    def gate_silu(self, st, w, j):
        b = self.projF(st, w, j)
        sg = self.ar_m.alloc(T * 2, BF16)
        self.P.op("act", lambda e: e.activation(out=sg.ap, in_=self.ps[b][:, :], func=AF.Silu), [self.t_ps[b]], sg.tts)
        return sg

    def bcast_chunk_last(self, ap2d):
        return ap2d.rearrange("p (c j) -> p c j", j=64)[:, :, 63:64].to_broadcast([128, 8, 64])

    def lin_attn_hp(self, st, pairs, QhT, KtT, vt, hp, dec_fn, s_idx, sg, ychunk, y, ar=None, inpar=False):
        P = self.P
        ar = ar or self.ar_m
        mark = ar.ptr
        kt_tok = ar.alloc(8 * 128 * 2, BF16)
        sbf = ar.alloc(8 * 128 * 2, BF16)
        sc_sb = ar.alloc(16 * 64 * 2, BF16)
        o_sb = ar.alloc(T * 4, F32)
        obf = ar.alloc(T * 2, BF16)
        dd = ar.alloc(T * 4, F32)
        d2 = ar.alloc(T * 2, BF16)
        rs = o_sb
        S = st.S[:, s_idx, :]
        tS = st.t_S[s_idx]
        ident = self.cb[:, C_IDENT:C_IDENT + 128]
        bt = self.bank("M")
        psb = self.ps[bt][:, :].bitcast(BF16)
        for c in range(8):
            P.op("pe", lambda e, c=c: e.transpose(out=psb[0:64, c * 128:(c + 1) * 128], in_=KtT.ap[:, c * 64:(c + 1) * 64],
                                                 identity=ident),
                 KtT.tts + [self.t_cb], [self.t_ps[bt]])
        P.op("act", lambda e: e.activation(out=kt_tok.ap[0:64, :], in_=psb[0:64, :], func=AF.Copy),
             [self.t_ps[bt]], kt_tok.tts)
        yield
        bU = [self.bank("M"), self.bank("M")]
        for c in range(8):
            b = bU[c // 4]
            self.mm(self.ps[b][:, (c % 4) * 128:(c % 4 + 1) * 128], kt_tok.ap[0:64, c * 128:(c + 1) * 128],
                    vt.ap[0:64, c * 256 + hp * 128: c * 256 + (hp + 1) * 128], True, True,
                    kt_tok.tts + vt.tts, [self.t_ps[b]])
        for c in range(8):
            b = bU[c // 4]
            P.op("dve", lambda e, c=c: e.tensor_copy(out=sbf.ap[:, c * 128:(c + 1) * 128], in_=S),
                 [tS], sbf.tts)
            dap, drd = dec_fn(c)
            P.op("dve", lambda e, c=c, b=b, dap=dap: e.scalar_tensor_tensor(
                out=S, in0=S, scalar=dap, in1=self.ps[b][:, (c % 4) * 128:(c % 4 + 1) * 128], op0=ALU.mult, op1=ALU.add),
                [tS, self.t_ps[b]] + drd, [tS])
            if c % 4 == 3 and (c == 7 or not inpar):
                yield 2
        first = True
        sc4 = sc_sb.ap[0:64, :].rearrange("p (c h j) -> p c h j", h=2, j=64)
        for (KT, QT, mask_fn) in pairs:
            bs = [self.bank("M"), self.bank("M")]
            for ph in range(2):
                b = bs[ph]
                for c in range(8):
                    self.mm(self.ps[b][0:64, c * 64:(c + 1) * 64], KT.ap[ph * 64:(ph + 1) * 64, c * 64:(c + 1) * 64],
                            QT.ap[ph * 64:(ph + 1) * 64, c * 64:(c + 1) * 64], True, True,
                            KT.tts + QT.tts, [self.t_ps[b]])
            for ph in range(2):
                b = bs[ph]
                src = self.ps[b][0:64, :].rearrange("p (c j) -> p c j", j=64)
                dst = sc4[:, :, ph, :]
                m = mask_fn(ph)
                if first:
                    P.op("dve", lambda e, src=src, dst=dst, m=m: e.tensor_tensor(out=dst, in0=src, in1=m, op=ALU.mult),
                         [self.t_ps[b], self.t_cf], sc_sb.tts)
                else:
                    tmp = ar.alloc(512 * 4, F32)
                    tv = tmp.ap[0:64, :].rearrange("p (c j) -> p c j", j=64)
                    P.op("dve", lambda e, src=src, tv=tv, m=m: e.tensor_tensor(out=tv, in0=src, in1=m, op=ALU.mult),
                         [self.t_ps[b], self.t_cf], tmp.tts)
                    P.op("pool", lambda e, dst=dst, tv=tv: e.tensor_tensor(out=dst, in0=dst, in1=tv, op=ALU.add),
                         tmp.tts + sc_sb.tts, sc_sb.tts)
            first = False
            yield
        bA = [self.bank("M"), self.bank("M")]
        bB1 = self.bank("M")
        for ph in range(2):
            for c in range(8):
                g = c * 2 + ph
                out = self.ps[bA[ph]][:, c * 64:(c + 1) * 64]
                self.mm(out, vt.ap[0:64, c * 256 + hp * 128: c * 256 + (hp + 1) * 128], sc_sb.ap[0:64, g * 64:(g + 1) * 64],
                        True, ph == 1, vt.tts + sc_sb.tts, [self.t_ps[bA[ph]]])
                if ph == 0:
                    self.mm(out, sbf.ap[0:64, c * 128:(c + 1) * 128], QhT.ap[0:64, c * 64:(c + 1) * 64],
                            False, True, sbf.tts + QhT.tts, [self.t_ps[bA[0]]])
        for c in range(8):
            self.mm(self.ps[bB1][:, c * 64:(c + 1) * 64], sbf.ap[64:128, c * 128:(c + 1) * 128], QhT.ap[64:128, c * 64:(c + 1) * 64],
                    True, True, sbf.tts + QhT.tts, [self.t_ps[bB1]])
        P.op("act", lambda e: e.activation(out=o_sb.ap[0:64, :], in_=self.ps[bA[0]][0:64, :], func=AF.Copy),
             [self.t_ps[bA[0]]], o_sb.tts)
        P.op("act", lambda e: e.activation(out=o_sb.ap[64:128, :], in_=self.ps[bA[1]][64:128, :], func=AF.Copy),
             [self.t_ps[bA[1]]], o_sb.tts)
        P.op("dve", lambda e: e.tensor_tensor(out=o_sb.ap[64:128, :], in0=o_sb.ap[64:128, :], in1=self.ps[bB1][64:128, :], op=ALU.add),
             [self.t_ps[bB1]] + o_sb.tts, o_sb.tts)
        P.op("dve", lambda e: e.tensor_copy(out=obf.ap, in_=o_sb.ap), o_sb.tts, obf.tts)
        yield
        bd64 = self.cb[:, C_BD64:C_BD64 + 128]
        bm = self.bank("M")
        self.mm(self.ps[bm][:, :], bd64, obf.ap, True, True, [self.t_cb] + obf.tts, [self.t_ps[bm]])
        P.op("dve", lambda e: e.tensor_tensor(out=dd.ap, in0=o_sb.ap, in1=self.ps[bm][:, :], op=ALU.subtract),
             o_sb.tts + [self.t_ps[bm]], dd.tts)
        P.op("act", lambda e: e.activation(out=d2.ap, in_=dd.ap, func=AF.Square), dd.tts, d2.tts)
        yield
        bv = self.bank("M")
        self.mm(self.ps[bv][:, :], bd64, d2.ap, True, True, [self.t_cb] + d2.tts, [self.t_ps[bv]])
        self.rsqrt_eps(rs.ap, self.ps[bv][:, :], [self.t_ps[bv]], rs.tts)
        P.op("dve", lambda e: e.tensor_tensor(out=dd.ap, in0=dd.ap, in1=rs.ap, op=ALU.mult), dd.tts + rs.tts, dd.tts)
        yd = y.ap[:, ychunk * T:(ychunk + 1) * T]
        P.op("pool", lambda e: e.tensor_tensor(out=yd, in0=dd.ap, in1=sg.ap, op=ALU.mult), dd.tts + sg.tts, [y.tts[ychunk]])
        ar.ptr = mark
        yield

    def mixer(self, st, l, base_i):
        P = self.P
        ar = self.ar_m
        ar.ptr = 0
        self.rmsnorm(st, l, P_MIX, ar, "M")
        ar.ptr = 0
        y = ar.alloc(KC * T * 2, BF16)
        if len(self.mixers) < 4:
            P.op("pool", lambda e: e.memset(y.ap, 0.0), [], y.tts)
        yield
        i = base_i
        mark0 = ar.ptr
        cos = st.cs[:, 0, :]
        sin = st.cs[:, 1, :]
        c3 = lambda ap: ap.rearrange("p (c j) -> p c j", j=64)
        if "ret" in self.mixers:
            ar.ptr = mark0
            rot = {}
            rtiles = {(nm, hp): ar.alloc(T * 2, BF16) for nm in ("q", "k") for hp in range(2)}
            mk_rope = ar.ptr
            for nm in ("q", "k"):
                ar.ptr = mk_rope
                w0 = self.wnext(("L", l, i)); i += 1
                b0 = [self.projF(st, w0, 0), self.projF(st, w0, 1)]
                self.wrel(w0)
                w1 = self.wnext(("L", l, i)); i += 1
                b1 = [self.projF(st, w1, 0), self.projF(st, w1, 1)]
                self.wrel(w1)
                for hp in range(2):
                    r = rtiles[(nm, hp)]
                    mkt = ar.ptr
                    t1 = ar.alloc(T * 4, F32)
                    t2 = ar.alloc(T * 4, F32)
                    ar.ptr = mkt if hp == 1 else ar.ptr
                    P.op("dve", lambda e, t1=t1, b=b0[hp]: e.tensor_tensor(out=t1.ap, in0=self.ps[b][:, :], in1=cos, op=ALU.mult),
                         [self.t_ps[b0[hp]], st.t_cs], t1.tts)
                    P.op("dve", lambda e, t2=t2, b=b1[hp]: e.tensor_tensor(out=t2.ap, in0=self.ps[b][:, :], in1=sin, op=ALU.mult),
                         [self.t_ps[b1[hp]], st.t_cs], t2.tts)
                    P.op("pool", lambda e, t1=t1, t2=t2, r=r: e.tensor_tensor(out=r.ap, in0=t1.ap, in1=t2.ap, op=ALU.add),
                         t1.tts + t2.tts, r.tts)
                    rot[(nm, hp)] = r
                yield
            ar.ptr = mk_rope
            wv = self.wnext(("L", l, i)); i += 1
            vt = ar.alloc(8 * 256 * 2, BF16)
            yield from self.projT(st, wv, vt)
            self.wrel(wv)
            wg = self.wnext(("L", l, i)); i += 1
            sgs_ = [self.gate_silu(st, wg, 0), self.gate_silu(st, wg, 1)]
            self.wrel(wg)
            yield

            def ret_task(hp, sub):
                sg = sgs_[hp]
                qh = sub.alloc(T * 2, BF16)
                kt = sub.alloc(T * 2, BF16)
                g1 = self.cf[:, C_G1 + hp * 64:C_G1 + (hp + 1) * 64].unsqueeze(1).to_broadcast([128, 8, 64])
                g2 = self.cf[:, C_G2 + hp * 64:C_G2 + (hp + 1) * 64].unsqueeze(1).to_broadcast([128, 8, 64])
                qr, kr = rot[("q", hp)], rot[("k", hp)]
                P.op("pool", lambda e: e.tensor_tensor(out=c3(qh.ap), in0=c3(qr.ap), in1=g1, op=ALU.mult),
                     qr.tts + [self.t_cf], qh.tts)
                P.op("pool", lambda e: e.tensor_tensor(out=c3(kt.ap), in0=c3(kr.ap), in1=g2, op=ALU.mult),
                     kr.tts + [self.t_cf], kt.tts)

                def mask_fn(ph):
                    h = hp * 2 + ph
                    return self.cf[0:64, C_DT + h * 64:C_DT + (h + 1) * 64].unsqueeze(1).to_broadcast([64, 8, 64])
                dec = self.cf[:, C_DEC + hp:C_DEC + hp + 1]
                yield
                yield from self.lin_attn_hp(st, [(kr, qr, mask_fn)], qh, kt, vt, hp, lambda c: (dec, [self.t_cf]),
                                            l * 6 + 0 + hp, sg, 0 + hp, y, ar=sub, inpar=True)
            subs = [ar.sub(14), ar.sub(14)]
            yield from par([ret_task(0, subs[0]), ret_task(1, subs[1])])
        else:
            i += 6
        if "lru" not in self.mixers:
            i += 2
        else:
            ar.ptr = mark0
            wx_ = self.wnext(("L", l, i)); i += 1
            wg_ = self.wnext(("L", l, i)); i += 1
            def lru_task(ch, sub):
                bx = self.projF(st, wx_, ch)
                xb = sub.alloc(516 * 4, F32)
                halo = st.halo[:, l * 2 + ch, 0:3]
                P.op("pool", lambda e, xb=xb, halo=halo: e.tensor_copy(out=xb.ap[:, 0:3], in_=halo), [st.t_halo[l]], xb.tts)
                P.op("act", lambda e, xb=xb, bx=bx: e.activation(out=xb.ap[:, 3:515], in_=self.ps[bx][:, :], func=AF.Copy),
                     [self.t_ps[bx]], xb.tts)
                P.op("pool", lambda e, xb=xb, halo=halo: e.tensor_copy(out=halo, in_=xb.ap[:, 512:515]), xb.tts, [st.t_halo[l]])
                xc = sub.alloc(T * 4, F32)
                cws = [self.pcol(l, P_CONVW + ch * 4 + tap) for tap in range(4)]
                P.op("pool", lambda e, xb=xb, xc=xc: e.tensor_scalar(out=xc.ap, in0=xb.ap[:, 3:515], scalar1=cws[3],
                                                                     scalar2=self.pcol(l, P_CONVB + ch), op0=ALU.mult, op1=ALU.add),
                     xb.tts + [self.t_par], xc.tts)
                for tap in range(3):
                    P.op("dve", lambda e, xb=xb, xc=xc, tap=tap: e.scalar_tensor_tensor(
                        out=xc.ap, in0=xb.ap[:, tap:tap + 512], scalar=cws[tap], in1=xc.ap, op0=ALU.mult, op1=ALU.add),
                        xb.tts + xc.tts + [self.t_par], xc.tts)
                xcb = sub.alloc(T * 2, BF16)
                P.op("act", lambda e, xc=xc, xcb=xcb: e.activation(out=xcb.ap, in_=xc.ap, func=AF.Copy), xc.tts, xcb.tts)
                yield 3
                br, bi = self.bank("M"), self.bank("M")
                wa = self.wsm[:, l * NWS + ch * 128: l * NWS + (ch + 1) * 128]
                wxm = self.wsm[:, l * NWS + 256 + ch * 128: l * NWS + 256 + (ch + 1) * 128]
                self.mm(self.ps[br][:, :], wa, xcb.ap, True, True, [self.t_wsm] + xcb.tts, [self.t_ps[br]])
                self.mm(self.ps[bi][:, :], wxm, xcb.ap, True, True, [self.t_wsm] + xcb.tts, [self.t_ps[bi]])
                r = sub.alloc(T * 4, F32)
                ig = sub.alloc(T * 4, F32)
                P.op("act", lambda e, r=r, br=br: e.activation(out=r.ap, in_=self.ps[br][:, :], func=AF.Sigmoid,
                                                               bias=self.pcol(l, P_BA + ch)), [self.t_ps[br], self.t_par], r.tts)
                P.op("act", lambda e, ig=ig, bi=bi: e.activation(out=ig.ap, in_=self.ps[bi][:, :], func=AF.Sigmoid,
                                                                 bias=self.pcol(l, P_BX + ch)), [self.t_ps[bi], self.t_par], ig.tts)
                a = sub.alloc(T * 4, F32)
                s = sub.alloc(T * 4, F32)
                P.op("act", lambda e, a=a, r=r: e.activation(out=a.ap, in_=r.ap, func=AF.Exp, scale=self.dcol(l, 4 + ch)),
                     r.tts + [self.t_der], a.tts)
                P.op("act", lambda e, s=s, r=r: e.activation(out=s.ap, in_=r.ap, func=AF.Exp, scale=self.dcol(l, 6 + ch)),
                     r.tts + [self.t_der], s.tts)
                P.op("act", lambda e, s=s: e.activation(out=s.ap, in_=s.ap, func=AF.Sqrt, scale=-1.0, bias=self.epsc[:, 1:2]),
                     s.tts + [self.t_epsc], s.tts)
                P.op("pool", lambda e, ig=ig, xc=xc: e.tensor_tensor(out=ig.ap, in0=ig.ap, in1=xc.ap, op=ALU.mult),
                     ig.tts + xc.tts, ig.tts)
                P.op("pool", lambda e, ig=ig, s=s: e.tensor_tensor(out=ig.ap, in0=ig.ap, in1=s.ap, op=ALU.mult),
                     ig.tts + s.tts, ig.tts)
                yield
                hcol = st.hst[:, l * 2 + ch: l * 2 + ch + 1]
                hh = sub.alloc(T * 4, F32)
                P.op("dve", lambda e, hh=hh, a=a, ig=ig, hcol=hcol: e.tensor_tensor_scan(
                    out=hh.ap, data0=a.ap, data1=ig.ap, initial=hcol, op0=ALU.mult, op1=ALU.add),
                    a.tts + ig.tts + [st.t_hst[l]], hh.tts)
                P.op("dve", lambda e, hh=hh, hcol=hcol: e.tensor_copy(out=hcol, in_=hh.ap[:, 511:512]), hh.tts, [st.t_hst[l]])
                bg = self.projF(st, wg_, ch)
                gx = sub.alloc(T * 4, F32)
                t3 = sub.alloc(T * 4, F32)
                P.op("act", lambda e, gx=gx, bg=bg: e.activation(out=gx.ap, in_=self.ps[bg][:, :], func=AF.Copy),
                     [self.t_ps[bg]], gx.tts)
                P.op("pool", lambda e, gx=gx, t3=t3: e.tensor_tensor(out=t3.ap, in0=gx.ap, in1=gx.ap, op=ALU.mult), gx.tts, t3.tts)
                P.op("pool", lambda e, t3=t3: e.tensor_scalar(out=t3.ap, in0=t3.ap, scalar1=0.044715, scalar2=1.0,
                                                              op0=ALU.mult, op1=ALU.add), t3.tts, t3.tts)
                P.op("pool", lambda e, gx=gx, t3=t3: e.tensor_tensor(out=t3.ap, in0=t3.ap, in1=gx.ap, op=ALU.mult),
                     t3.tts + gx.tts, t3.tts)
                P.op("act", lambda e, t3=t3: e.activation(out=t3.ap, in_=t3.ap, func=AF.Sigmoid, scale=1.5957691216057308),
                     t3.tts, t3.tts)
                P.op("pool", lambda e, gx=gx, t3=t3: e.tensor_tensor(out=t3.ap, in0=t3.ap, in1=gx.ap, op=ALU.mult),
                     t3.tts + gx.tts, t3.tts)
                yd = y.ap[:, (2 + ch) * T:(3 + ch) * T]
                P.op("pool", lambda e, hh=hh, t3=t3, yd=yd: e.tensor_tensor(out=yd, in0=hh.ap, in1=t3.ap, op=ALU.mult),
                     hh.tts + t3.tts, [y.tts[2 + ch]])
                yield
            subs = [ar.sub(20), ar.sub(20)]
            yield from par([lru_task(0, subs[0]), lru_task(1, subs[1])])
            self.wrel(wx_)
            self.wrel(wg_)
        for nm in ("gla", "hgrn"):
            if nm not in self.mixers:
                i += 5 if nm == "gla" else 4
                continue
            ar.ptr = mark0
            gla = nm == "gla"
            wq = self.wnext(("L", l, i)); i += 1
            wk = self.wnext(("L", l, i)); i += 1
            wv = self.wnext(("L", l, i)); i += 1
            vt = ar.alloc(8 * 256 * 2, BF16)
            yield from self.projT(st, wv, vt)
            self.wrel(wv)
            if gla:
                wa_ = self.wnext(("L", l, i)); i += 1
                ba_ = self.projF(st, wa_, 0)
                self.wrel(wa_)
                alr = ar.alloc(T * 2, BF16)
                P.op("act", lambda e: e.activation(out=alr.ap[0:16, :], in_=self.ps[ba_][0:16, :], func=AF.Copy),
                     [self.t_ps[ba_]], alr.tts)
            wg = self.wnext(("L", l, i)); i += 1
            es = -1.0 / 16 if gla else 1.0
            for hp in range(2):
                mk = ar.ptr
                sg = self.gate_silu(st, wg, hp)
                bq = self.projF(st, wq, hp)
                bk = self.projF(st, wk, hp)
                if hp == 1:
                    self.wrel(wg); self.wrel(wq); self.wrel(wk)
                B = ar.alloc(T * 4, F32)
                lf = ar.alloc(T * 4, F32)
                if gla:
                    bp = self.bank("M")
                    wa2 = self.wsm[0:16, l * NWS + 512 + hp * 128: l * NWS + 512 + (hp + 1) * 128]
                    self.mm(self.ps[bp][:, :], wa2, alr.ap[0:16, :], True, True, [self.t_wsm] + alr.tts, [self.t_ps[bp]])
                    P.op("act", lambda e: e.activation(out=lf.ap, in_=self.ps[bp][:, :], func=AF.Exp, scale=-1.0,
                                                       bias=self.dcol(l, 8 + hp)), [self.t_ps[bp], self.t_der], lf.tts)
                    P.op("act", lambda e: e.activation(out=lf.ap, in_=lf.ap, func=AF.Ln, bias=self.epsc[:, 1:2]),
                         lf.tts + [self.t_epsc], lf.tts)
                    kk = None
                else:
                    kk = ar.alloc(T * 4, F32)
                    P.op("act", lambda e: e.activation(out=lf.ap, in_=self.ps[bk][:, :], func=AF.Sigmoid),
                         [self.t_ps[bk]], lf.tts)
                    P.op("dve", lambda e: e.tensor_scalar(out=lf.ap, in0=lf.ap, scalar1=self.dcol(l, 2 + hp),
                                                          scalar2=self.dcol(l, 0 + hp), op0=ALU.mult, op1=ALU.add),
                         lf.tts + [self.t_der], lf.tts)
                    P.op("pool", lambda e: e.tensor_scalar(out=kk.ap, in0=lf.ap, scalar1=-1.0, scalar2=1.0,
                                                           op0=ALU.mult, op1=ALU.add), lf.tts, kk.tts)
                    P.op("act", lambda e: e.activation(out=lf.ap, in_=lf.ap, func=AF.Ln), lf.tts, lf.tts)
                reset = self.cf[:, C_RESET:C_RESET + 512]
                P.op("dve", lambda e: e.tensor_tensor_scan(out=B.ap, data0=reset, data1=lf.ap, initial=0.0,
                                                           op0=ALU.mult, op1=ALU.add), lf.tts + [self.t_cf], B.tts)
                yield 2
                E = ar.alloc(T * 4, F32)
                dl = lf
                P.op("act", lambda e: e.activation(out=E.ap, in_=B.ap, func=AF.Exp, scale=es), B.tts, E.tts)
                P.op("pool", lambda e: e.tensor_tensor(out=c3(dl.ap), in0=c3(B.ap), in1=self.bcast_chunk_last(B.ap),
                                                       op=ALU.subtract), B.tts, dl.tts)
                if not gla:
                    P.op("dve", lambda e: e.tensor_scalar(out=dl.ap, in0=dl.ap, scalar1=80.0, scalar2=None, op0=ALU.min),
                         dl.tts, dl.tts)
                Eq = ar.alloc(T * 4, F32)
                Ek = ar.alloc(T * 4, F32)
                P.op("act", lambda e: e.activation(out=Eq.ap, in_=dl.ap, func=AF.Exp, scale=es), dl.tts, Eq.tts)
                P.op("act", lambda e: e.activation(out=Ek.ap, in_=dl.ap, func=AF.Exp, scale=-es), dl.tts, Ek.tts)
                Qp = ar.alloc(T * 2, BF16)
                Qh = ar.alloc(T * 2, BF16)
                Km = ar.alloc(T * 2, BF16)
                if gla:
                    qsc = 32.0 ** -0.5
                    Qm = ar.alloc(T * 2, BF16)
                    Kp = ar.alloc(T * 2, BF16)
                    for (dst, fac) in ((Qp, Eq), (Qm, Ek), (Qh, E)):
                        P.op("dve", lambda e, dst=dst, fac=fac: e.scalar_tensor_tensor(
                            out=dst.ap, in0=self.ps[bq][:, :], scalar=qsc, in1=fac.ap, op0=ALU.mult, op1=ALU.mult),
                            [self.t_ps[bq]] + fac.tts, dst.tts)
                    for (dst, fac) in ((Km, Ek), (Kp, Eq)):
                        P.op("dve", lambda e, dst=dst, fac=fac: e.tensor_tensor(
                            out=dst.ap, in0=self.ps[bk][:, :], in1=fac.ap, op=ALU.mult),
                            [self.t_ps[bk]] + fac.tts, dst.tts)
                    ml = lambda ph: self.cf[0:64, C_ML:C_ML + 64].unsqueeze(1).to_broadcast([64, 8, 64])
                    mu = lambda ph: self.cf[0:64, C_MU:C_MU + 64].unsqueeze(1).to_broadcast([64, 8, 64])
                    pairs = [(Km, Qp, ml), (Kp, Qm, mu)]
                else:
                    qs = ar.alloc(T * 4, F32)
                    P.op("act", lambda e: e.activation(out=qs.ap, in_=self.ps[bq][:, :], func=AF.Silu),
                         [self.t_ps[bq]], qs.tts)
                    P.op("dve", lambda e: e.tensor_tensor(out=Qp.ap, in0=qs.ap, in1=Eq.ap, op=ALU.mult),
                         qs.tts + Eq.tts, Qp.tts)
                    P.op("dve", lambda e: e.tensor_tensor(out=Qh.ap, in0=qs.ap, in1=E.ap, op=ALU.mult),
                         qs.tts + E.tts, Qh.tts)
                    P.op("pool", lambda e: e.tensor_tensor(out=Km.ap, in0=kk.ap, in1=Ek.ap, op=ALU.mult),
                         kk.tts + Ek.tts, Km.tts)
                    ml = lambda ph: self.cf[0:64, C_ML:C_ML + 64].unsqueeze(1).to_broadcast([64, 8, 64])
                    pairs = [(Km, Qp, ml)]
                dec_fn = lambda c, E=E: (E.ap[:, c * 64 + 63: c * 64 + 64], E.tts)
                yield 3
                yield from self.lin_attn_hp(st, pairs, Qh, Km, vt, hp, dec_fn, l * 6 + (2 if gla else 4) + hp, sg,
                                            (4 if gla else 6) + hp, y)
                ar.ptr = mk
        for cp in range(4):
            w = self.wnext(("L", l, i)); i += 1
            for j in range(2):
                dc = cp * 2 + j
                b = self.bank("M")
                for ec in range(KC):
                    self.mm(self.ps[b][:, :], w.ap[:, ec * 256 + j * 128: ec * 256 + (j + 1) * 128],
                            y.ap[:, ec * T:(ec + 1) * T], ec == 0, ec == KC - 1, w.tts + [y.tts[ec]], [self.t_ps[b]])
                P.op("dve", lambda e, dc=dc, b=b: e.tensor_tensor(out=st.x[:, dc, :], in0=self.ps[b][:, :],
                                                                  in1=st.x[:, dc, :], op=ALU.add),
                     [self.t_ps[b], st.t_x[dc]], [st.t_x[dc]])
            self.wrel(w)
            yield

    def load_tile(self, st, k):
        st.cur_k = k
        tok0 = st.s * SEQ + k * T
        src = self.dr["xT"][:, tok0:tok0 + T].rearrange("(kc p) t -> p kc t", p=128)
        self.dma("sp", st.x[:, 0:4, :], src[:, 0:4, :], [], st.t_x[0:4], st.s)
        self.dma("sp", st.x[:, 4:8, :], src[:, 4:8, :], [], st.t_x[4:8], 6 + st.s)
        if "mix" in self.subl and "ret" in self.mixers:
            self.dma("sp", st.cs[:, :, :], self.dr["rope"][:, :, k * T:(k + 1) * T].rearrange("a p t -> p a t"), [],
                     [st.t_cs], 2 + st.s)
        yield

    def final_store(self, st, k):
        P = self.P
        ar = self.ar_f
        ar.ptr = 0
        tok0 = st.s * SEQ + k * T
        dst = self.dr["outT"][:, tok0:tok0 + T].rearrange("(kc p) t -> p kc t", p=128)
        if self.final_norm:
            ob = ar.alloc(KC * T * 4, F32)
            sq = ar.alloc(KC * T * 2, BF16)
            rstd = ar.alloc(T * 4, F32)
            for half in range(2):
                src = st.x[:, half * 4:(half + 1) * 4, :]
                d_ = sq.ap[:, half * 4 * T:(half + 1) * 4 * T].rearrange("p (a b) -> p a b", b=T)
                P.op("act", lambda e, src=src, d_=d_: e.activation(out=d_, in_=src, func=AF.Square),
                     st.t_x[half * 4:(half + 1) * 4], sq.tts[half * 4:(half + 1) * 4])
            b = self.bank("F")
            for kc in range(KC):
                self.mm(self.ps[b][:, :], self.cb[:, C_AVGD:C_AVGD + 128], sq.ap[:, kc * T:(kc + 1) * T],
                        kc == 0, kc == KC - 1, [self.t_cb, sq.tts[kc]], [self.t_ps[b]])
            self.rsqrt_eps(rstd.ap, self.ps[b][:, :], [self.t_ps[b]], rstd.tts)
            for kc in range(KC):
                g = self.pcol(0, P_FIN + kc)
                P.op("dve", lambda e, kc=kc, g=g: e.scalar_tensor_tensor(out=ob.ap[:, kc * T:(kc + 1) * T], in0=st.x[:, kc, :],
                                                                        scalar=g, in1=rstd.ap, op0=ALU.mult, op1=ALU.mult),
                     [st.t_x[kc], self.t_par] + rstd.tts, ob.tts[kc * 2:kc * 2 + 2])
            srcap = ob.ap.rearrange("p (a b) -> p a b", b=T)
            rd = ob.tts
        else:
            srcap = st.x[:, :, :]
            rd = st.t_x
        self.st_cnt[st.s] += 16
        P.op("act", lambda e: e.dma_start(out=dst, in_=srcap), rd, [], dma=(self.st_sems[st.s], self.st_cnt[st.s]))
        yield

    def stream_phases(self, st):
        n_ffn = N_FFN
        base_mix = n_ffn
        base_xa = n_ffn + 17 + 4
        base_f2 = base_xa + 8
        phases = []

        def chain(*gens):
            def run():
                for g in gens:
                    yield from g()
            return run

        cur = [lambda st=st: self.load_tile(st, 0)]
        for k in range(self.tps):
            for l in range(self.n_layers):
                if "ffn1" in self.subl:
                    cur.append(lambda st=st, l=l: self.ffn(st, l, P_FFN1, 0))
                phases.append(("P", chain(*cur)))
                Ls = []
                if "mix" in self.subl:
                    Ls.append(lambda st=st, l=l: self.mixer(st, l, base_mix))
                if "xa" in self.subl:
                    Ls.append(lambda st=st, l=l: self.xattn(st, l, base_xa))
                phases.append(("L", chain(*Ls)))
                cur = []
                if "ffn2" in self.subl:
                    cur.append(lambda st=st, l=l: self.ffn(st, l, P_FFN2, base_f2))
            cur.append(lambda st=st, k=k: self.final_store(st, k))
            if k + 1 < self.tps:
                cur.append(lambda st=st, k=k: self.load_tile(st, k + 1))
        phases.append(("P", chain(*cur)))
        return phases

    def run_all(self, key, gen):
        n = 0.0
        for w in gen:
            n += (w or 1)
        if self.dry and self.unit_counts is None:
            self.rec_units[key] = max(n, 1)

    def interleave(self, keyL, gL, keyP, gP):
        if self.dry and self.unit_counts is None:
            self.run_all(keyL, gL)
            self.run_all(keyP, gP)
            return
        nL, nP = self.unit_counts[keyL], self.unit_counts[keyP]
        acc = 0.0
        doneL = doneP = False
        while not (doneL and doneP):
            if not doneL:
                w = 1
                try:
                    w = next(gL) or 1
                except StopIteration:
                    doneL = True
                acc += w * nP / nL
            else:
                acc = 1e9
            while acc >= 1.0 and not doneP:
                try:
                    next(gP)
                except StopIteration:
                    doneP = True
                acc -= 1.0
            if doneP and not doneL:
                acc = 0.0

    def emit_all(self):
        P = self.P
        self.prologue()
        ph = [self.stream_phases(self.streams[s]) for s in range(self.n_streams)]
        if self.n_streams == 1:
            for i, (t, f) in enumerate(ph[0]):
                self.run_all((0, i), f())
                if i == 0 and "xa" in self.subl:
                    self.kv_prepass(0)
        else:
            A, B = ph
            self.run_all((0, 0), A[0][1]())
            if "xa" in self.subl:
                for s in range(self.n_streams):
                    self.kv_prepass(s)
            ia, ib = 1, 0
            while ia < len(A) or ib < len(B):
                if ia < len(A) and ib < len(B):
                    ta, tb = A[ia][0], B[ib][0]
                    assert ta != tb
                    if ta == "L":
                        self.interleave((0, ia), A[ia][1](), (1, ib), B[ib][1]())
                    else:
                        self.interleave((1, ib), B[ib][1](), (0, ia), A[ia][1]())
                    ia += 1
                    ib += 1
                elif ia < len(A):
                    self.run_all((0, ia), A[ia][1]()); ia += 1
                else:
                    self.run_all((1, ib), B[ib][1]()); ib += 1
        for s in range(self.n_streams):
            sem_, cnt_ = self.st_sems[s], self.st_cnt[s]
            if cnt_:
                P.op("sp", lambda e, sem_=sem_, cnt_=cnt_: e.wait_ge(sem_, cnt_), [], [])


_CACHE = {}


def pack_weights(inp, builder):
    NPL = builder.NPL
    out = np.zeros((builder.NPIECES * 128, PIECE), np.float32)
    for l in range(DEPTH):
        for i, (nm, k0, nk, cols) in enumerate(builder.lspecs):
            idx = l * NPL + i
            out[idx * 128:(idx + 1) * 128] = pack_piece(inp[nm][l], k0, nk, cols)
        for i, (nm, k0, nk, cols) in enumerate(builder.kspecs):
            idx = DEPTH * NPL + l * 8 + i
            out[idx * 128:(idx + 1) * 128] = pack_piece(inp[nm][l], k0, nk, cols)
    return out


def make_in_maps(inp, builder, ncores=NCORES):
    inp = {k: np.asarray(v, dtype=np.float32) for k, v in inp.items()}
    wsrc = pack_weights(inp, builder)
    consts = make_constants()
    rope = make_rope_tables()
    params = np.ascontiguousarray(np.concatenate([make_params(inp, l) for l in range(DEPTH)], axis=1))
    wsmall = np.ascontiguousarray(np.concatenate([make_wsmall(inp, l) for l in range(DEPTH)], axis=1))
    maps = []
    for c in range(ncores):
        xs = inp["x"][c * SEQ_PER_CORE:(c + 1) * SEQ_PER_CORE].reshape(SEQ_PER_CORE * SEQ, D)
        xT = np.ascontiguousarray(xs.T)
        memT = np.ascontiguousarray(inp["mem"][c * SEQ_PER_CORE:(c + 1) * SEQ_PER_CORE].transpose(0, 2, 1))
        maps.append({"xT": xT, "memT": memT, "wsrc": wsrc, "consts": consts, "rope": rope,
                     "params": params, "wsmall": wsmall})
    return maps


def kernel(**inputs):
    if "builder" not in _CACHE:
        b = Builder()
        _CACHE["builder"] = b
        _CACHE["nc"] = b.build()
    b = _CACHE["builder"]
    nc = _CACHE["nc"]
    maps = make_in_maps(inputs, b)
    res = run_bass_kernel_spmd(nc, maps, core_ids=list(range(NCORES)))
    outs = []
    for c in range(NCORES):
        oT = np.asarray(res.results[c]["outT"])
        outs.append(oT.T.reshape(SEQ_PER_CORE, SEQ, D))
    return np.ascontiguousarray(np.concatenate(outs, axis=0).astype(np.float32))
```
